# Optimizing a Trainium2 kernel written in Bass

```python
import jax, jax.numpy as jnp
from jax import lax
import numpy as np

D_MODEL = 1024
BATCH = 2
SEQ = 16384
DEPTH = 1

GRID_W = 64
CTX_LEN = 256
RET_HEADS = 4
RET_QK_DIM = 256
RET_V_DIM = 512
RET_CHUNK = 128
CONV_DIM = D_MODEL
CONV_WIDTH = 31
D_FF = 2816
ROPE_BASE = 10000.0
EPS = 1e-6
N_MOD = 9

RET_QK_W = RET_HEADS * RET_QK_DIM
RET_V_W = RET_HEADS * RET_V_DIM
K0 = RET_QK_W
V0 = 2 * RET_QK_W
G0 = V0 + RET_V_W
C0 = G0 + RET_V_W
GA0 = C0 + 2 * CONV_DIM
GB0 = GA0 + D_MODEL
IN_WIDTH = GB0 + D_MODEL
IN_SPLITS = (K0, V0, G0, C0, GA0, GB0)

kernel_name = 'hybrid_retention_conformer_dit'

F32 = jnp.float32


def rmsnorm(x, w):
    xf = x.astype(F32)
    y = xf * lax.rsqrt(jnp.mean(xf * xf, axis=-1, keepdims=True) + EPS)
    return (y * w.astype(F32)).astype(x.dtype)


def layernorm(x, w, b):
    xf = x.astype(F32)
    mu = jnp.mean(xf, axis=-1, keepdims=True)
    var = jnp.mean(jnp.square(xf - mu), axis=-1, keepdims=True)
    y = (xf - mu) * lax.rsqrt(var + EPS)
    return (y * w.astype(F32) + b.astype(F32)).astype(x.dtype)


def modulate(x, w, shift, scale):
    return rmsnorm(x, w) * (1 + scale) + shift


def swiglu(x, w_in, w_out):
    a, b = jnp.split(x @ w_in, 2, axis=-1)
    return (jax.nn.silu(a) * b) @ w_out


def heads(t, d):
    return t.reshape(t.shape[0], t.shape[1], RET_HEADS, d)


def rotate_block(x, ang):
    x1, x2 = jnp.split(x, 2, axis=-1)
    cos = jnp.cos(ang)[None, :, None, :]
    sin = jnp.sin(ang)[None, :, None, :]
    return jnp.concatenate([x1 * cos - x2 * sin, x1 * sin + x2 * cos], axis=-1)


def rope_2d(x, rows, cols):
    half = x.shape[-1] // 2
    inv = ROPE_BASE ** (-jnp.arange(0, half, 2, dtype=F32) / half)
    xr, xc = jnp.split(x.astype(F32), 2, axis=-1)
    out = jnp.concatenate([rotate_block(xr, rows[:, None] * inv[None, :]),
                           rotate_block(xc, cols[:, None] * inv[None, :])], axis=-1)
    return out.astype(x.dtype)


def retention_dir(q, k, v, log_g, state0, strict):
    b, L, h, dk = q.shape
    dv = v.shape[-1]
    n = L // RET_CHUNK
    qc = q.astype(F32).reshape(b, n, RET_CHUNK, h, dk)
    kc = k.astype(F32).reshape(b, n, RET_CHUNK, h, dk)
    vc = v.astype(F32).reshape(b, n, RET_CHUNK, h, dv)
    pos = jnp.arange(RET_CHUNK, dtype=F32)
    diff = pos[:, None] - pos[None, :]
    mask = (diff > 0) if strict else (diff >= 0)
    intra = jnp.where(mask[None], jnp.exp(log_g[:, None, None] * jnp.maximum(diff, 0.0)[None]), 0.0)
    scores = jnp.einsum('bnchd,bnshd->bnhcs', qc, kc) * intra[None, None]
    inner = jnp.einsum('bnhcs,bnshe->bnche', scores, vc)
    q_dec = jnp.exp(log_g[None, :] * (pos[:, None] + 1.0))
    k_dec = jnp.exp(log_g[None, :] * (RET_CHUNK - 1.0 - pos)[:, None])
    c_dec = jnp.exp(log_g * RET_CHUNK)

    def step(state, xs):
        qi, ki, vi = xs
        cross = jnp.einsum('bchd,bhde->bche', qi, state) * q_dec[None, :, :, None]
        upd = jnp.einsum('bshd,bshe->bhde', ki * k_dec[None, :, :, None], vi)
        return state * c_dec[None, :, None, None] + upd, cross

    state, cross = lax.scan(step, state0, (qc.swapaxes(0, 1), kc.swapaxes(0, 1), vc.swapaxes(0, 1)))
    out = (inner + cross.swapaxes(0, 1)).reshape(b, L, h, dv)
    return out.astype(v.dtype), state


def bidir_retention(q, k, v, log_gf, log_gb, rf0, rb0):
    out_f, rf = retention_dir(q, k, v, log_gf, rf0, False)
    out_b, rb = retention_dir(q[:, ::-1], k[:, ::-1], v[:, ::-1], log_gb, rb0, True)
    return out_f + out_b[:, ::-1], rf, rb


def context_state(k, v, log_g, reverse):
    L = k.shape[1]
    pos = jnp.arange(L, dtype=F32)
    expo = pos if reverse else (L - 1.0) - pos
    w = jnp.exp(expo[:, None] * log_g[None, :])
    return jnp.einsum('blhd,blhe->bhde', k.astype(F32) * w[None, :, :, None], v.astype(F32))


def retention_branch(o, gate, gn_w, w_o):
    b, L, h, dv = o.shape
    of = o.astype(F32)
    mu = jnp.mean(of, axis=-1, keepdims=True)
    var = jnp.mean(jnp.square(of - mu), axis=-1, keepdims=True)
    of = ((of - mu) * lax.rsqrt(var + EPS)).reshape(b, L, h * dv) * gn_w.astype(F32)
    return (jax.nn.silu(gate) * of.astype(gate.dtype)) @ w_o


def conv_branch(z, conv_w, conv_b, ln_w, ln_b, w_pw):
    a, g = jnp.split(z, 2, axis=-1)
    y = a * jax.nn.sigmoid(g)
    pad = CONV_WIDTH // 2
    y = lax.conv_general_dilated(y, conv_w[:, None, :].astype(y.dtype), window_strides=(1,),
                                 padding=[(pad, pad)], dimension_numbers=('NWC', 'WIO', 'NWC'),
                                 feature_group_count=CONV_DIM) + conv_b
    y = jax.nn.silu(layernorm(y, ln_w, ln_b))
    return y @ w_pw


def mixer_output(ret_out, rg, cv, ga, gb, gn_w, w_ret_o, conv_w, conv_b, ln_w, ln_b, w_conv_o, w_o):
    y_ret = retention_branch(ret_out, rg, gn_w, w_ret_o)
    y_conv = conv_branch(cv, conv_w, conv_b, ln_w, ln_b, w_conv_o)
    return (jax.nn.sigmoid(ga) * y_ret + jax.nn.sigmoid(gb) * y_conv) @ w_o


def setup_inputs(seed: int = 0) -> dict:
    key = jax.random.key(seed)
    ks = jax.random.split(key, 32)

    def nrm(k, shape, scale):
        return jax.random.normal(k, shape, F32) * scale

    decay_logit = jnp.asarray(np.log(2.0 ** (5.0 + np.arange(RET_HEADS)) - 1.0), F32)
    return {
        'x': nrm(ks[0], (BATCH, SEQ, D_MODEL), 1.0),
        'c': nrm(ks[1], (BATCH, D_MODEL), 1.0),
        'ctx': nrm(ks[2], (BATCH, CTX_LEN, D_MODEL), 1.0),
        'c_ctx': nrm(ks[3], (D_MODEL,), 1.0),
        'w_mod': nrm(ks[4], (DEPTH, D_MODEL, N_MOD * D_MODEL), 0.5 * D_MODEL ** -0.5),
        'b_mod': nrm(ks[5], (DEPTH, N_MOD * D_MODEL), 0.01),
        'norm_ffn1': 1.0 + nrm(ks[6], (DEPTH, D_MODEL), 0.01),
        'w_ffn1_in': nrm(ks[7], (DEPTH, D_MODEL, 2 * D_FF), D_MODEL ** -0.5),
        'w_ffn1_out': nrm(ks[8], (DEPTH, D_FF, D_MODEL), D_FF ** -0.5),
        'norm_mix': 1.0 + nrm(ks[9], (DEPTH, D_MODEL), 0.01),
        'w_in': nrm(ks[10], (DEPTH, D_MODEL, IN_WIDTH), D_MODEL ** -0.5),
        'ret_decay_f': decay_logit[None, :] + nrm(ks[11], (DEPTH, RET_HEADS), 0.1),
        'ret_decay_b': decay_logit[None, :] + nrm(ks[12], (DEPTH, RET_HEADS), 0.1),
        'ret_gn_w': 1.0 + nrm(ks[13], (DEPTH, RET_V_W), 0.01),
        'w_ret_out': nrm(ks[14], (DEPTH, RET_V_W, D_MODEL), RET_V_W ** -0.5),
        'conv_w': nrm(ks[15], (DEPTH, CONV_WIDTH, CONV_DIM), CONV_WIDTH ** -0.5),
        'conv_b': nrm(ks[16], (DEPTH, CONV_DIM), 0.01),
        'conv_ln_w': 1.0 + nrm(ks[17], (DEPTH, CONV_DIM), 0.01),
        'conv_ln_b': nrm(ks[18], (DEPTH, CONV_DIM), 0.01),
        'w_conv_out': nrm(ks[19], (DEPTH, CONV_DIM, D_MODEL), CONV_DIM ** -0.5),
        'w_out': nrm(ks[20], (DEPTH, D_MODEL, D_MODEL), D_MODEL ** -0.5),
        'norm_ffn2': 1.0 + nrm(ks[21], (DEPTH, D_MODEL), 0.01),
        'w_ffn2_in': nrm(ks[22], (DEPTH, D_MODEL, 2 * D_FF), D_MODEL ** -0.5),
        'w_ffn2_out': nrm(ks[23], (DEPTH, D_FF, D_MODEL), D_FF ** -0.5),
        'final_norm': 1.0 + nrm(ks[24], (D_MODEL,), 0.01),
    }


def reference(x, c, ctx, c_ctx, w_mod, b_mod, norm_ffn1, w_ffn1_in, w_ffn1_out, norm_mix, w_in,
              ret_decay_f, ret_decay_b, ret_gn_w, w_ret_out, conv_w, conv_b, conv_ln_w, conv_ln_b,
              w_conv_out, w_out, norm_ffn2, w_ffn2_in, w_ffn2_out, final_norm):
    B, L, _ = x.shape
    ROWS = L // GRID_W
    rows = jnp.repeat(jnp.arange(ROWS, dtype=F32), GRID_W)
    cols = jnp.tile(jnp.arange(GRID_W, dtype=F32), ROWS)
    qk_scale = RET_QK_DIM ** -0.5
    h, hc = x, ctx
    for layer in range(DEPTH):
        last = layer + 1 == DEPTH
        mods = jnp.split((jax.nn.silu(c) @ w_mod[layer] + b_mod[layer])[:, None, :], N_MOD, axis=-1)
        mods_c = jnp.split(jax.nn.silu(c_ctx) @ w_mod[layer] + b_mod[layer], N_MOD, axis=-1)
        sh1, sc1, g1, sh2, sc2, g2, sh3, sc3, g3 = mods
        sh1c, sc1c, g1c, sh2c, sc2c, g2c, sh3c, sc3c, g3c = mods_c

        h = h + 0.5 * g1 * swiglu(modulate(h, norm_ffn1[layer], sh1, sc1), w_ffn1_in[layer], w_ffn1_out[layer])
        hc = hc + 0.5 * g1c * swiglu(modulate(hc, norm_ffn1[layer], sh1c, sc1c), w_ffn1_in[layer], w_ffn1_out[layer])

        u = modulate(h, norm_mix[layer], sh2, sc2)
        uc = modulate(hc, norm_mix[layer], sh2c, sc2c)
        log_gf = jax.nn.log_sigmoid(ret_decay_f[layer].astype(F32))
        log_gb = jax.nn.log_sigmoid(ret_decay_b[layer].astype(F32))
        branch_w = (ret_gn_w[layer], w_ret_out[layer], conv_w[layer], conv_b[layer], conv_ln_w[layer],
                    conv_ln_b[layer], w_conv_out[layer], w_out[layer])

        if last:
            kv_c = uc @ w_in[layer][:, K0:G0]
            kc_, vc_ = jnp.split(kv_c, [RET_QK_W], axis=-1)
            kc_ = heads(kc_, RET_QK_DIM) * qk_scale
            vc_ = heads(vc_, RET_V_DIM)
            rf = context_state(kc_, vc_, log_gf, False)
            rb = context_state(kc_, vc_, log_gb, True)
        else:
            qc_, kc_, vc_, rgc, cvc, gac, gbc = jnp.split(uc @ w_in[layer], IN_SPLITS, axis=-1)
            zeros = jnp.zeros((B, RET_HEADS, RET_QK_DIM, RET_V_DIM), F32)
            oc, rf, rb = bidir_retention(heads(qc_, RET_QK_DIM), heads(kc_, RET_QK_DIM) * qk_scale,
                                         heads(vc_, RET_V_DIM), log_gf, log_gb, zeros, zeros)
            hc = hc + g2c * mixer_output(oc, rgc, cvc, gac, gbc, *branch_w)
            hc = hc + 0.5 * g3c * swiglu(modulate(hc, norm_ffn2[layer], sh3c, sc3c), w_ffn2_in[layer], w_ffn2_out[layer])

        q, k, v, rg, cv, ga, gb = jnp.split(u @ w_in[layer], IN_SPLITS, axis=-1)
        q = rope_2d(heads(q, RET_QK_DIM), rows, cols)
        k = rope_2d(heads(k, RET_QK_DIM), rows, cols) * qk_scale
        v = heads(v, RET_V_DIM)
        o, _, _ = bidir_retention(q, k, v, log_gf, log_gb, rf, rb)
        h = h + g2 * mixer_output(o, rg, cv, ga, gb, *branch_w)

        h = h + 0.5 * g3 * swiglu(modulate(h, norm_ffn2[layer], sh3, sc3), w_ffn2_in[layer], w_ffn2_out[layer])
    return rmsnorm(h, final_norm)
```

```python
import numpy as np
from concourse.bass_utils import run_bass_kernel_spmd
import concourse.bass as bass
import concourse.mybir as mybir

F32 = mybir.dt.float32
BF16 = mybir.dt.bfloat16
ALU = mybir.AluOpType
AF = mybir.ActivationFunctionType
ENGS = ['pe', 'act', 'dve', 'pool', 'sp']


def _flat(x):
    o = []
    for k in x:
        if isinstance(k, (list, tuple)): o.extend(_flat(k))
        else: o.append(k)
    return o


class _Rec:
    def __init__(self): self.calls = []
    def __getattr__(self, name):
        def call(*a, **kw):
            self.calls.append((name, a, kw)); return None
        return call


class Ins:
    __slots__ = ('eng', 'fn', 'deps', 'is_dma', 'sem', 'val', 'prev', 'signal', 'cc')

    def __init__(self, eng, fn, is_dma, cc=False):
        self.eng = eng; self.fn = fn; self.is_dma = is_dma; self.deps = []
        self.sem = None; self.val = 0; self.prev = 0; self.signal = False; self.cc = cc


class Prog:
    def __init__(self):
        self.streams = {e: [] for e in ENGS}
        self.bufs = {}
        self.bar = {e: None for e in ENGS}
        self.all_dma = []

    def op(self, eng, fn, reads=(), writes=(), dma=False, cc=False):
        rec_ = _Rec(); fn(rec_); calls_ = rec_.calls
        assert calls_, 'op recorded no calls'
        def fn(e, calls_=calls_):
            r = None
            for nm_, a_, kw_ in calls_:
                r = getattr(e, nm_)(*a_, **kw_)
            return r
        ins = Ins(eng, fn, dma or cc, cc)
        reads = _flat(reads); writes = _flat(writes)
        writes = writes + [k for k in reads if isinstance(k, str) and k.startswith('ps') and k not in writes]
        raw = set(); other = set()
        for k in reads:
            st = self.bufs.get(k)
            if st: raw.update(st['w'])
        for k in writes:
            st = self.bufs.get(k)
            if st:
                other.update(st['w']); other.update(st['r'])
        if self.bar[eng] is not None:
            raw.update(self.bar[eng]); self.bar[eng] = None
        deps = set(raw)
        for d in other:
            if d.eng == eng and not d.is_dma and not ins.is_dma and eng == 'pe':
                continue
            if d.eng == eng and not d.is_dma and ins.is_dma and False:
                continue
            deps.add(d)
        if eng == 'pe':
            deps = {d for d in deps if not (d.eng == 'pe' and not d.is_dma)}
        deps.discard(ins)
        ins.deps = list(deps)
        for d in ins.deps: d.signal = True
        for k in reads:
            st = self.bufs.setdefault(k, {'w': [], 'r': []})
            st['r'].append(ins)
        for k in writes:
            st = self.bufs.get(k)
            if st is None or st['r']:
                self.bufs[k] = {'w': [ins], 'r': []}
            else:
                st['w'].append(ins)
        self.streams[eng].append(ins)
        if ins.is_dma and not ins.cc: self.all_dma.append(ins)
        return ins

    def barrier(self):
        last = []
        for e in ENGS:
            s = self.streams[e]
            for ins in reversed(s):
                if not ins.is_dma:
                    last.append(ins); break
        last.extend(self.all_dma)
        self.all_dma = []
        for e in ENGS: self.bar[e] = list(last) + (self.bar[e] or [])
        self.bufs = {k: v for k, v in self.bufs.items() if isinstance(k, str) and k.startswith('COUT')}

    def emit(self, nc, npool=24):
        from contextlib import ExitStack
        with ExitStack() as es:
            esem = {e: es.enter_context(nc.semaphore('es_' + e)) for e in ENGS}
            pools = {e: [es.enter_context(nc.semaphore('dp_%s_%d' % (e, i))) for i in range(npool)]
                     for e in ('sp', 'act', 'pool')}
            ccsem = es.enter_context(nc.semaphore('ccsem'))
            finals = {}
            ccn = 0
            for e in ENGS:
                cnt = 0; rr = 0; cum = {}
                for ins in self.streams[e]:
                    if ins.cc:
                        ccn += 1; ins.sem = ccsem; ins.val = ccn; ins.prev = 0
                    elif ins.is_dma:
                        s = pools[e][rr % npool]; rr += 1
                        ins.sem = s; ins.prev = cum.get(id(s), 0)
                        cum[id(s)] = ins.prev + 16; ins.val = ins.prev + 16
                        finals[id(s)] = (s, ins.val)
                    elif ins.signal:
                        cnt += 1; ins.sem = esem[e]; ins.val = cnt
            self.maxcnt = {e: sum(1 for i in self.streams[e] if i.signal and not i.is_dma) for e in ENGS}
            block = es.enter_context(nc.Block())

            def run(e, eng):
                waited = {}
                for ins in self.streams[e]:
                    for d in ins.deps:
                        if waited.get(id(d.sem), 0) < d.val:
                            eng.wait_ge(d.sem, d.val); waited[id(d.sem)] = d.val
                    if ins.is_dma and not ins.cc and ins.prev > 0 and waited.get(id(ins.sem), 0) < ins.prev:
                        eng.wait_ge(ins.sem, ins.prev); waited[id(ins.sem)] = ins.prev
                    r = ins.fn(eng)
                    if ins.cc:
                        r.then_inc(ins.sem)
                    elif ins.is_dma:
                        r.then_inc(ins.sem, 16)
                    elif ins.signal:
                        r.then_inc(ins.sem, 1)
                if e == 'sp':
                    for s, v in finals.values():
                        if waited.get(id(s), 0) < v: eng.wait_ge(s, v)

            @block.tensor
            def _(eng): run('pe', eng)

            @block.scalar
            def _(eng): run('act', eng)

            @block.vector
            def _(eng): run('dve', eng)

            @block.gpsimd
            def _(eng): run('pool', eng)

            @block.sync
            def _(eng): run('sp', eng)


DM_ = 1024; T = 4096; TC = 256; TA = T + TC; DFF = 2816; KF = 22
TILES = [(i * 512, 512) for i in range(8)] + [(T, TC)]
MAIN = TILES[:8]
K0, V0, G0, C0, GA0, GB0 = 1024, 2048, 4096, 6144, 8192, 9216
NPC = 3096
EPS = 1e-6


class Arena:
    def __init__(self, ap, n): self.ap = ap; self.n = n; self.off = 0; self.reg = []; self.subs = {}; self.limit = n
    def reset(self): self.off = 0; self.reg = []; self.subs = {}
    def f32(self, n, _reg=True):
        assert self.off + n <= min(self.n, self.limit), ("arena overflow", self.off + n, self.limit)
        a = self.ap[:, self.off:self.off + n]
        if _reg: self.reg.append((a, 'A%d' % self.off))
        self.off += n
        return a
    def bf16(self, n):
        assert n % 2 == 0
        off = self.off
        a = self.f32(n // 2, _reg=False).bitcast(BF16)
        self.reg.append((a, 'A%d' % off))
        return a
    def f32_sub(self, base, off, n):
        v = base[:, off:off + n]
        k = self.key(base) + '_s%d' % off
        self.reg.append((v, k)); self.subs.setdefault(self.key(base), []).append(k)
        return v
    def keys_all(self, base):
        return [self.key(base)] + self.subs.get(self.key(base), [])
    def alias(self, view, base):
        self.reg.append((view, self.key(base))); return view
    def key(self, a):
        for o, k in self.reg:
            if o is a: return k
        raise KeyError("unregistered buffer")


class _Stop(Exception):
    pass


def build_nc(stage=99, dumps=()):
    try:
        return _build_nc(stage, dumps)
    except _Stop as e_:
        return e_.args[0]


def _build_nc(stage=99, dumps=()):
    from contextlib import ExitStack
    nc = bass.Bass("TRN2", target_bir_lowering=False)
    def EI(name, shape): return nc.dram_tensor(name, list(shape), F32, kind="ExternalInput").ap()
    xin = EI("xin", [TA, DM_]); smalls = EI("smalls", [512, 128]); gnw_d = EI("gnw", [1, 2048])
    decay_d = EI("decay", [1, 8]); ropetab = EI("ropetab", [4, 128, TA]); pconst = EI("pconst", [128, NPC])
    ident_d = EI("ident", [128, 128]); cconst = EI("cconst", [1, 32])
    w_mod = EI("w_mod", [DM_, 9216]); w1i = EI("w_ffn1_in", [DM_, 2 * DFF]); w1o = EI("w_ffn1_out", [DFF, DM_])
    w_in = EI("w_in", [DM_, 10240]); w_ro = EI("w_ret_out", [2048, DM_]); w_co = EI("w_conv_out", [DM_, DM_])
    w_o = EI("w_out", [DM_, DM_]); w2i = EI("w_ffn2_in", [DM_, 2 * DFF]); w2o = EI("w_ffn2_out", [DFF, DM_])
    out_d = nc.dram_tensor("out", [T, DM_], F32, kind="ExternalOutput").ap()
    def DT(name, shape, dt): return nc.dram_tensor(name, list(shape), dt).ap()
    XF = DT("XF", [8, 128, TA], F32); XT = DT("XT", [8, 128, TA], BF16); HT = DT("HT", [KF, 128, TA], BF16)
    H1 = DT("H1", [8, 128, TA], F32); UT = DT("UT", [8, 128, TA], BF16)
    QTR = DT("QTR", [8, 128, T], BF16); QTF = DT("QTF", [8, 128, T], BF16); QTB = DT("QTB", [8, 128, T], BF16)
    KTR = DT("KTR", [8, 128, TA], BF16); VV = DT("VV", [TA, 2048], BF16); RG = DT("RG", [T, 2048], BF16)
    YT = DT("YT", [8, 128, T + 32], BF16); SGA = DT("SGA", [8, 128, T], F32); SGB = DT("SGB", [8, 128, T], F32)
    UU = DT("UU", [8, 2, 8, 128, 512], F32); SF = DT("SF", [8, 8, 128, 512], BF16); SB = DT("SB", [8, 8, 128, 512], BF16)
    ZT = DT("ZT", [16, 128, T], BF16); M1 = DT("M1", [8, 128, T], F32); CT = DT("CT", [8, 128, T], BF16)
    MT = DT("MT", [8, 128, T], BF16); H2 = DT("H2", [8, 128, T], F32)
    CINS = [DT("CIN%d" % i, [256, 512], F32) for i in range(8)]; COUTS = [DT("COUT%d" % i, [4 * 256, 512], F32) for i in range(8)]
    CINH = DT("CINH", [2048, 16], BF16); COUTH = DT("COUTH", [4 * 2048, 16], BF16)
    RCX = DT("RCX", [2, 8, 128, 512], F32)
    QD_D = DT("QD_D", [2, 128, 2048], F32); DMT_D = DT("DMT_D", [128, 8192], F32); DG_D = DT("DG_D", [128, 248 * 128], BF16)
    P = Prog()
    es = ExitStack()
    with es:
        def SBT(name, n, dt=F32): return es.enter_context(nc.sbuf_tensor("s_" + name, [128, n], dt))[:, :]
        ident = SBT("ident", 128); identb = SBT("identb", 128, BF16); ones = SBT("ones", 128)
        SM = SBT("SM", 512); MODS = SBT("MODS", 144); DER = SBT("DER", 2 * 5 * 8); LG = SBT("LG", 8)
        KDF = SBT("KDF", 16); KDB = SBT("KDB", 16); GEF = SBT("GEF", 32); GEB = SBT("GEB", 32); C512 = SBT("C512", 8)
        COF = SBT("COF", 16); COB = SBT("COB", 16); CTXC = SBT("CTXC", 8); CC = SBT("CC", 32)
        NAR = 33280 + 14336
        ARt = SBT("arena", NAR); AR = Arena(ARt, NAR)
        PS = [es.enter_context(nc.psum_tensor("ps%d" % i, [128, 512], F32))[:, :] for i in range(8)]
        pbc = [0]
        def bank():
            i = pbc[0] % 8; pbc[0] += 1
            return i
        uid = [0]
        def U(s):
            uid[0] += 1; return "%s_%d" % (s, uid[0])
        KB = lambda a: AR.key(a)
        def dma(out, in_, reads=(), writes=(), eng='sp'):
            return P.op(eng, lambda e, o=out, i=in_: e.dma_start(out=o, in_=i), reads=reads, writes=writes, dma=True)
        rot = [0]
        def ew():
            rot[0] += 1
            return ('dve', 'pool')[rot[0] % 2]
        def copy_any(eng, out, in_, reads, writes):
            if eng == 'act':
                P.op('act', lambda e: e.activation(out, in_, AF.Copy), reads=reads, writes=writes)
            else:
                P.op(eng, lambda e: e.tensor_copy(out, in_), reads=reads, writes=writes)
        def MODv(j, kc, w): return MODS[:, (j * 8 + kc) * 2 + w:(j * 8 + kc) * 2 + w + 1]
        def DERv(w, i, kc): return DER[:, (w * 5 + i) * 8 + kc:(w * 5 + i) * 8 + kc + 1]
        def rstd_from(psv, W, outv, tmpv, kin, kout, ktmp):
            P.op('act', lambda e: e.activation(tmpv, psv, AF.Ln, bias=EPS, scale=1.0), reads=kin, writes=[ktmp])
            P.op('act', lambda e: e.activation(outv, tmpv, AF.Exp, scale=-0.5), reads=[ktmp], writes=[kout])

        sbuf_named = {}
        dram_named = dict(XF=XF, XT=XT, HT=HT, H1=H1, UT=UT, QTR=QTR, QTF=QTF, QTB=QTB, KTR=KTR, VV=VV, RG=RG, YT=YT, SGA=SGA, SGB=SGB,
                          UU=UU, SF=SF, SB=SB, ZT=ZT, M1=M1, CT=CT, MT=MT, H2=H2, RCX=RCX, COUTH=COUTH, **{'COUT%d' % i: COUTS[i] for i in range(8)})
        def stage_end(n):
            if stage != n: return
            P.barrier()
            for nm in dumps:
                if nm in dram_named:
                    src = dram_named[nm]
                    dst = nc.dram_tensor("dbg_" + nm, list(src.shape), src.dtype, kind="ExternalOutput").ap()
                    dma(dst, src)
                else:
                    src = sbuf_named[nm]
                    dst = nc.dram_tensor("dbg_" + nm, list(src.shape), src.dtype, kind="ExternalOutput").ap()
                    dma(dst, src)
            P.emit(nc)
            raise _Stop(nc)
        dma(ident, ident_d, writes=['ident'])
        P.op('dve', lambda e: e.tensor_copy(identb, ident), reads=['ident'], writes=['identb'])
        P.op('pool', lambda e: e.memset(ones, 1.0 / 1024.0), writes=['ones'])
        smt = AR.f32(512)
        for b in range(4):
            dma(smt[:, b * 128:(b + 1) * 128], smalls[b * 128:(b + 1) * 128, :], writes=['smt%d' % b])
        for b in range(4):
            P.op('pe', lambda e, b=b: e.transpose(PS[0][:, b * 128:(b + 1) * 128], smt[:, b * 128:(b + 1) * 128], ident),
                 reads=['smt%d' % b, 'ident'], writes=['ps0'])
        P.op('dve', lambda e: e.tensor_copy(SM, PS[0]), reads=['ps0'], writes=['SM'])
        ccin = AR.f32(16)
        ccv = ccin.rearrange("p (k w) -> p k w", w=2)
        P.op('act', lambda e: e.activation(ccv[:, :, 0], SM[:, 72:80], AF.Silu), reads=['SM'], writes=['cc'])
        P.op('act', lambda e: e.activation(ccv[:, :, 1], SM[:, 80:88], AF.Silu), reads=['SM'], writes=['cc'])
        wst = [AR.f32(8192), AR.f32(8192)]
        DG0 = AR.bf16(248 * 128)
        for k in range(31):
            for kc in range(8):
                j = k * 8 + kc; cw = SM[:, 144 + k * 8 + kc:145 + k * 8 + kc]
                if j % 2 == 0:
                    P.op('dve', lambda e: e.tensor_scalar(DG0[:, j * 128:(j + 1) * 128], identb, cw, None, ALU.mult), reads=['identb', 'SM'], writes=['DG0'])
                else:
                    P.op('act', lambda e: e.activation(DG0[:, j * 128:(j + 1) * 128], identb, AF.Copy, scale=cw), reads=['identb', 'SM'], writes=['DG0'])
        pm = bank()
        for j in range(9):
            ws = wst[j % 2]; wsv = ws.rearrange("p (k c) -> p k c", k=8)
            for kc in range(8):
                dma(wsv[:, kc, :], w_mod[kc * 128:(kc + 1) * 128, j * 1024:(j + 1) * 1024], writes=['wst%d' % (j % 2)])
            def f(e, j=j, wsv=wsv):
                r = None
                for oc in range(8):
                    for kc in range(8):
                        c = (j * 8 + oc) * 2
                        r = e.matmul(PS[pm][:, c:c + 2], wsv[:, kc, oc * 128:(oc + 1) * 128], ccv[:, kc, :],
                                     start=(kc == 0), stop=(kc == 7))
                return r
            P.op('pe', f, reads=['wst%d' % (j % 2), 'cc'], writes=['ps%d' % pm])
        for q4 in range(4):
            dma(DG_D[:, q4 * 7936:(q4 + 1) * 7936], DG0[:, q4 * 7936:(q4 + 1) * 7936], reads=['DG0'])
        psmv = PS[pm][:, 0:144].rearrange("p (c w) -> p c w", w=2)
        modsv = MODS.rearrange("p (c w) -> p c w", w=2)
        for w in range(2):
            P.op('dve', lambda e, w=w: e.tensor_tensor(modsv[:, :, w], psmv[:, :, w], SM[:, 0:72], ALU.add),
                 reads=['ps%d' % pm, 'SM'], writes=['MODS'])
        tmp8 = AR.f32(8)
        for w in range(2):
            for i, (jsc, ncol) in enumerate([(1, 88), (None, None), (4, 96), (7, 104), (None, None)]):
                dv = DER[:, (w * 5 + i) * 8:(w * 5 + i) * 8 + 8]
                if jsc is None:
                    jg = 2 if i == 1 else 8
                    src = modsv[:, jg * 8:(jg + 1) * 8, w]
                    P.op('dve', lambda e, dv=dv, src=src: e.tensor_scalar(dv, src, 0.5, None, ALU.mult), reads=['MODS'], writes=['DER'])
                else:
                    src = modsv[:, jsc * 8:(jsc + 1) * 8, w]
                    k1 = KB(tmp8)
                    P.op('dve', lambda e, src=src: e.tensor_scalar(tmp8, src, 1.0, None, ALU.add), reads=['MODS'], writes=[k1])
                    P.op('dve', lambda e, dv=dv, ncol=ncol: e.tensor_tensor(dv, tmp8, SM[:, ncol:ncol + 8], ALU.mult),
                         reads=[k1, 'SM'], writes=['DER'])
        P.barrier(); AR.reset()
        dcy = AR.f32(8); dce = AR.f32(8)
        dma(dcy, decay_d.partition_broadcast(128), writes=['dcy'])
        dma(CC, cconst.partition_broadcast(128), writes=['CC'])
        P.op('act', lambda e: e.activation(dce, dcy, AF.Exp, scale=-1.0), reads=['dcy'], writes=['dce'])
        P.op('act', lambda e: e.activation(dcy, dce, AF.Ln, bias=1.0, scale=1.0), reads=['dce'], writes=['dcl'])
        P.op('dve', lambda e: e.tensor_scalar(LG, dcy, -1.0, None, ALU.mult), reads=['dcl'], writes=['LG'])
        pc = AR.f32(NPC); QDF = AR.f32(2048); QDB = AR.f32(2048); DMT = AR.f32(8192)
        dma(pc, pconst, writes=['pc'])
        def expo(out, in_, col, wk):
            P.op('act', lambda e: e.activation(out, in_, AF.Exp, scale=LG[:, col:col + 1]), reads=['LG', 'pc', 'CC'], writes=[wk])
        for h in range(4):
            expo(KDF.rearrange("p (s h) -> p s h", h=4)[:, :, h], pc[:, 0:4], h, 'KD')
            expo(KDB.rearrange("p (s h) -> p s h", h=4)[:, :, h], pc[:, 4:8], 4 + h, 'KD')
            expo(GEF.rearrange("p (s h) -> p s h", h=4)[:, :, h], pc[:, 8:16], h, 'GE')
            expo(GEB.rearrange("p (s h) -> p s h", h=4)[:, :, h], pc[:, 16:24], 4 + h, 'GE')
            expo(QDF[:, h * 512:(h + 1) * 512], pc[:, 24:536], h, 'QD')
            expo(QDB[:, h * 512:(h + 1) * 512], pc[:, 536:1048], 4 + h, 'QD')
            expo(COF.rearrange("p (r h) -> p r h", h=4)[:, :, h], CC[:, 0:4], h, 'COF0')
            expo(COB.rearrange("p (r h) -> p r h", h=4)[:, :, h], CC[:, 8:12], 4 + h, 'COB0')
            expo(CTXC[:, h:h + 1], CC[:, 16:17], h, 'CTXC')
            expo(CTXC[:, 4 + h:5 + h], CC[:, 17:18], 4 + h, 'CTXC')
        P.op('act', lambda e: e.activation(C512, LG, AF.Exp, scale=512.0), reads=['LG'], writes=['C512'])
        for h in range(4):
            P.op('dve', lambda e, h=h: e.tensor_tensor(COF.rearrange("p (r h) -> p r h", h=4)[:, :, h],
                                                      COF.rearrange("p (r h) -> p r h", h=4)[:, :, h], CC[:, 4:8], ALU.mult),
                 reads=['COF0', 'CC'], writes=['COF'])
            P.op('dve', lambda e, h=h: e.tensor_tensor(COB.rearrange("p (r h) -> p r h", h=4)[:, :, h],
                                                      COB.rearrange("p (r h) -> p r h", h=4)[:, :, h], CC[:, 12:16], ALU.mult),
                 reads=['COB0', 'CC'], writes=['COB'])
        delta = pc[:, 1048:3096]
        dp = AR.f32(2048); dn = AR.f32(2048); mge = AR.f32(2048); mlt = AR.f32(2048); fb = AR.f32(2048); bb = AR.f32(2048)
        P.op('dve', lambda e: e.tensor_scalar(dp, delta, 0.0, None, ALU.max), reads=['pc'], writes=['dp'])
        P.op('dve', lambda e: e.tensor_scalar(dn, delta, -1.0, 0.0, ALU.mult, ALU.max), reads=['pc'], writes=['dn'])
        P.op('pool', lambda e: e.tensor_single_scalar(mge, delta, 0.0, ALU.is_ge), reads=['pc'], writes=['mge'])
        P.op('pool', lambda e: e.tensor_single_scalar(mlt, delta, 0.0, ALU.is_lt), reads=['pc'], writes=['mlt'])
        for h in range(4):
            kf = KB(fb); kb = KB(bb)
            P.op('act', lambda e, h=h: e.activation(fb, dp, AF.Exp, scale=LG[:, h:h + 1]), reads=['dp', 'LG'], writes=[kf])
            P.op('act', lambda e, h=h: e.activation(bb, dn, AF.Exp, scale=LG[:, 4 + h:5 + h]), reads=['dn', 'LG'], writes=[kb])
            P.op('dve', lambda e: e.tensor_tensor(fb, fb, mge, ALU.mult), reads=[kf, 'mge'], writes=[kf])
            P.op('pool', lambda e: e.tensor_tensor(bb, bb, mlt, ALU.mult), reads=[kb, 'mlt'], writes=[kb])
            P.op('dve', lambda e, h=h: e.tensor_tensor(DMT[:, h * 2048:(h + 1) * 2048], fb, bb, ALU.add),
                 reads=[kf, kb], writes=['DMT'])
        dma(QD_D[0], QDF, reads=['QD']); dma(QD_D[1], QDB, reads=['QD']); dma(DMT_D, DMT, reads=['DMT'])
        P.barrier(); AR.reset()

        def norm_A(xf, W, kx, sqb):
            xfv = xf.rearrange("p (k t) -> p k t", k=8); sqv = sqb.rearrange("p (k t) -> p k t", k=8)
            ksq = KB(sqb)
            P.op('act', lambda e: e.activation(sqv[:, :, 0:W], xfv[:, :, 0:W], AF.Square), reads=[kx], writes=[ksq])
            b = bank()
            def f(e):
                r = None
                for kc in range(8):
                    r = e.matmul(PS[b][:, 0:W], ones, sqv[:, kc, 0:W], start=(kc == 0), stop=(kc == 7))
                return r
            P.op('pe', f, reads=[ksq, 'ones'], writes=['ps%d' % b])
            return b
        def norm_B(xf, W, Af, Bf, dstXT, t0, kx, ms, kms, rs, tmps, xm):
            xfv = xf.rearrange("p (k t) -> p k t", k=8); xmv = xm.rearrange("p (k t) -> p k t", k=8)
            krs = KB(rs)
            rstd_from(ms, W, rs[:, 0:W], tmps[0][:, 0:W], kms, krs, KB(tmps[0]))
            kxm = KB(xm)
            for kc in range(8):
                tp = tmps[1 + kc % 2]; kt = KB(tp)
                P.op('dve', lambda e, kc=kc, tp=tp: e.tensor_tensor(tp[:, 0:W], xfv[:, kc, 0:W], rs[:, 0:W], ALU.mult),
                     reads=[kx, krs], writes=[kt])
                P.op('act', lambda e, kc=kc, tp=tp: e.activation(xmv[:, kc, 0:W], tp[:, 0:W], AF.Identity, bias=Bf(kc), scale=Af(kc)),
                     reads=[kt, 'MODS', 'DER'], writes=[kxm])
            dma(dstXT[:, :, t0:t0 + W].rearrange("k p t -> p k t"), xmv[:, :, 0:W], reads=[kxm])
        def norm_tile(xf, W, Af, Bf, dstXT, t0, kx, sqb, rs, tmps, xm):
            b = norm_A(xf, W, kx, sqb)
            norm_B(xf, W, Af, Bf, dstXT, t0, kx, PS[b][:, 0:W], ['ps%d' % b], rs, tmps, xm)

        def norm_pass(src_fm, dstXT, tiles, ai, bj, first=False, next_w=None):
            P.barrier(); AR.reset()
            set_limit({0} if next_w is not None else set())
            bg = None; per = 0
            if next_w is not None:
                KCn, nun, accsn = next_w
                bg = wgen(0, KCn, nun, accsn, make_stg()); per = -(-nchunks(KCn, nun, accsn) // len(tiles))
            assert first
            xfs = [AR.f32(4096), AR.f32(4096)]; sqbs = [AR.f32(4096), AR.f32(4096)]; rs = AR.f32(512); mss = [AR.f32(512), AR.f32(512)]
            tmps = [AR.f32(512) for _ in range(3)]; xms = [AR.bf16(4096), AR.bf16(4096)]
            xtoks = [AR.f32(4096), AR.f32(4096)]
            def stageA(ti):
                t0, W = tiles[ti]; ns = W // 128
                xf = xfs[ti % 2]; xfv = xf.rearrange("p (k t) -> p k t", k=8); kx = KB(xf)
                xtok = xtoks[ti % 2]; xtv = xtok.rearrange("p (s d) -> p s d", d=1024); ktk = KB(xtok)
                dma(xtv[:, 0:ns, :], xin[t0:t0 + W, :].rearrange("(s p) d -> p s d", p=128), writes=[ktk])
                for kc in range(8):
                    b = bank()
                    def f(e, kc=kc, b=b):
                        r = None
                        for s_ in range(ns):
                            r = e.transpose(PS[b][:, s_ * 128:(s_ + 1) * 128], xtv[:, s_, kc * 128:(kc + 1) * 128], ident)
                        return r
                    P.op('pe', f, reads=[ktk, 'ident'], writes=['ps%d' % b])
                    copy_any(('act', 'dve')[kc % 2], xfv[:, kc, 0:W], PS[b][:, 0:W], ['ps%d' % b], [kx])
                dma(XF[:, :, t0:t0 + W].rearrange("k p t -> p k t"), xfv[:, :, 0:W], reads=[kx])
                b = norm_A(xf, W, kx, sqbs[ti % 2])
                ms = mss[ti % 2]
                P.op('dve', lambda e: e.tensor_copy(ms[:, 0:W], PS[b][:, 0:W]), reads=['ps%d' % b], writes=[KB(ms)])
            def stageB(ti):
                t0, W = tiles[ti]; w = 1 if t0 >= T else 0
                xf = xfs[ti % 2]; ms = mss[ti % 2]
                norm_B(xf, W, lambda kc: DERv(w, ai, kc), lambda kc: MODv(bj, kc, w), dstXT, t0, KB(xf), ms[:, 0:W], [KB(ms)], rs, tmps, xms[ti % 2])
            stageA(0)
            for ti in range(len(tiles)):
                if ti + 1 < len(tiles): stageA(ti + 1)
                stageB(ti)
                if bg is not None: drain(bg, per)
            drain(bg)

        SLOTW = 11264
        SLOTS = [ARt[:, NAR - (i + 1) * SLOTW:NAR - i * SLOTW].bitcast(BF16) for i in range(2)]
        SKEY = ['SLOT0', 'SLOT1']
        def set_limit(live):
            AR.limit = NAR - 2 * SLOTW if 1 in live else (NAR - SLOTW if 0 in live else NAR)
        def wviews(slot, KC, nu, naccs):
            return [SLOTS[slot][:, a * KC * nu * 128:(a + 1) * KC * nu * 128].rearrange("p (k c) -> p k c", k=KC) for a in range(naccs)]
        def wgen(slot, KC, nu, accs, stg):
            wkey = SKEY[slot]; views = wviews(slot, KC, nu, len(accs))
            chunks = []
            for ai_, (Wd, col0, scale, mode) in enumerate(accs):
                for kc in range(KC):
                    for c0 in range(0, nu * 128, 1024):
                        chunks.append((views[ai_], Wd, kc, col0, c0, min(1024, nu * 128 - c0), scale, mode))
            L = len(stg); D = L - 1
            def dma_chunk(i):
                v, Wd, kc, col0, c0, n, scale, mode = chunks[i]; sb = stg[i % L]
                dma(sb[:, 0:n], Wd[kc * 128:(kc + 1) * 128, col0 + c0:col0 + c0 + n], writes=[KB(sb)])
            def cast_chunk(i):
                v, Wd, kc, col0, c0, n, scale, mode = chunks[i]; sb = stg[i % L]; ks = KB(sb)
                o = v[:, kc, c0:c0 + n]
                if mode == 'plain':
                    if i % 2 == 0:
                        P.op('act', lambda e: e.activation(o, sb[:, 0:n], AF.Copy, scale=float(scale)), reads=[ks], writes=[wkey])
                    else:
                        P.op('dve', lambda e: e.tensor_scalar(o, sb[:, 0:n], float(scale), None, ALU.mult), reads=[ks], writes=[wkey])
                else:
                    ov = o.rearrange("p (b h c) -> p b h c", h=2, c=64); sv = sb[:, 0:n].rearrange("p (b h c) -> p b h c", h=2, c=64)
                    P.op('dve', lambda e: e.tensor_scalar(ov[:, :, 0, :], sv[:, :, 1, :], -float(scale), None, ALU.mult), reads=[ks], writes=[wkey])
                    P.op('act', lambda e: e.activation(ov[:, :, 1, :], sv[:, :, 0, :], AF.Copy, scale=float(scale)), reads=[ks], writes=[wkey])
            for i in range(min(D, len(chunks))): dma_chunk(i)
            for i in range(len(chunks)):
                if i + D < len(chunks): dma_chunk(i + D)
                cast_chunk(i)
                yield len(chunks) - i - 1
        def drain(g, n=None):
            if g is None: return
            k = 0
            for _ in g:
                k += 1
                if n is not None and k >= n: return
        def nchunks(KC, nu, accs): return len(accs) * KC * ((nu * 128 + 1023) // 1024)
        def make_stg():
            stgall = AR.f32(4096)
            AR.stgall = stgall
            return [AR.f32_sub(stgall, i * 1024, 1024) for i in range(4)]

        def gemm_fm(SRC, KC, groups, tiles, evac, pre_tile=None, post_tile=None, extra_alloc=None, prefetch=True,
                    slot=0, preloaded=False, next_w=None):
            P.barrier(); AR.reset()
            G = len(groups)
            live = {(slot + g) % 2 for g in range(G)}
            nslot = (slot + G) % 2
            if next_w is not None: live.add(nslot)
            set_limit(live)
            stg = make_stg()
            xts = [AR.bf16(KC * 512), AR.bf16(KC * 512)]
            ctx = extra_alloc() if extra_alloc else None
            if not preloaded:
                drain(wgen(slot, KC, groups[0][0], groups[0][1], stg))
            for gi, (nu, accs) in enumerate(groups):
                gslot = (slot + gi) % 2
                wkey = SKEY[gslot]
                WBs = wviews(gslot, KC, nu, len(accs))
                bg = None; per = 0
                if gi + 1 < G:
                    bg = wgen((slot + gi + 1) % 2, KC, groups[gi + 1][0], groups[gi + 1][1], stg)
                    per = -(-nchunks(KC, groups[gi + 1][0], groups[gi + 1][1]) // len(tiles))
                elif next_w is not None:
                    KCn, nun, accsn = next_w
                    bg = wgen(nslot, KCn, nun, accsn, stg); per = -(-nchunks(KCn, nun, accsn) // len(tiles))
                def issue(ti):
                    t0, W = tiles[ti]
                    xt = xts[ti % 2].rearrange("p (k t) -> p k t", k=KC); kx = KB(xts[ti % 2])
                    dma(xt[:, :, 0:W], SRC[:, :, t0:t0 + W].rearrange("k p t -> p k t"), writes=[kx])
                    if pre_tile: pre_tile(ti, t0, W, gi, ctx)
                ahead = prefetch
                if ahead: issue(0)
                for ti, (t0, W) in enumerate(tiles):
                    xt = xts[ti % 2].rearrange("p (k t) -> p k t", k=KC); kx = KB(xts[ti % 2])
                    if ahead:
                        if ti + 1 < len(tiles): issue(ti + 1)
                    else:
                        issue(ti)
                    for u in range(nu):
                        pss = []
                        for ai_ in range(len(accs)):
                            b = bank()
                            def f(e, b=b, v=WBs[ai_], u=u, xt=xt, W=W):
                                r = None
                                for kc in range(KC):
                                    r = e.matmul(PS[b][:, 0:W], v[:, kc, u * 128:(u + 1) * 128], xt[:, kc, 0:W],
                                                 start=(kc == 0), stop=(kc == KC - 1))
                                return r
                            P.op('pe', f, reads=[wkey, kx], writes=['ps%d' % b])
                            pss.append(b)
                        evac(ti, t0, W, gi, u, pss, ctx)
                        if bg is not None and per and u == nu // 2: drain(bg, (per + 1) // 2)
                    if bg is not None and per: drain(bg, per - (per + 1) // 2 if per > 1 else 0)
                    if post_tile: post_tile(ti, t0, W, gi, ctx)
                drain(bg)

        def gemm_tm(SRC, Wd, col0, tiles, evac, slot=0, preloaded=False, next_w=None):
            P.barrier(); AR.reset()
            live = {slot}
            if next_w is not None: live.add(1 - slot)
            set_limit(live)
            stg = make_stg()
            WB = wviews(slot, 8, 16, 1)[0]
            xts = [AR.bf16(4096), AR.bf16(4096)]
            sts = [AR.bf16(512) for _ in range(4)]
            AR.tm_tmp = [AR.f32(512) for _ in range(4)]; AR.tm_gnw = AR.f32(2048)
            dma(AR.tm_gnw, gnw_d.partition_broadcast(128), writes=[KB(AR.tm_gnw)])
            wkey = SKEY[slot]
            if not preloaded:
                drain(wgen(slot, 8, 16, [(Wd, col0, 1.0, 'plain')], stg))
            bg = None; per = 0
            if next_w is not None:
                KCn, nun, accsn = next_w
                bg = wgen(1 - slot, KCn, nun, accsn, stg); per = -(-nchunks(KCn, nun, accsn) // len(tiles))
            n = [0]
            def ld_x(ti):
                t0, W = tiles[ti]
                dma(xts[ti % 2].rearrange("p (k t) -> p k t", k=8)[:, :, 0:W], SRC[:, :, t0:t0 + W].rearrange("k p t -> p k t"), writes=[KB(xts[ti % 2])])
            ld_x(0)
            for ti, (t0, W) in enumerate(tiles):
                xt = xts[ti % 2].rearrange("p (k t) -> p k t", k=8); kx = KB(xts[ti % 2])
                if ti + 1 < len(tiles): ld_x(ti + 1)
                for s in range(W // 128):
                    for cb in range(4):
                        b = bank()
                        def f(e, b=b, s=s, cb=cb, xt=xt):
                            r = None
                            for kc in range(8):
                                r = e.matmul(PS[b], xt[:, kc, s * 128:(s + 1) * 128], WB[:, kc, cb * 512:(cb + 1) * 512],
                                             start=(kc == 0), stop=(kc == 7))
                            return r
                        P.op('pe', f, reads=[wkey, kx], writes=['ps%d' % b])
                        st = sts[n[0] % 4]; n[0] += 1
                        evac(t0, s, cb, b, st)
                    if bg is not None and per: drain(bg, -(-per // (W // 128)))
            drain(bg)
        def ffn(XTsrc, w_i, w_o_, tiles, RES, gi_, OUT, final=False, norm_next=None, pre_in=True):
            def alloc1():
                return {'tmp': [AR.f32(512) for _ in range(3)], 'st': [AR.bf16(512) for _ in range(3)], 'n': [0]}
            def ev1(ti, t0, W, gi, u, pss, c):
                i = c['n'][0] % 3; c['n'][0] += 1
                tmp = c['tmp'][i]; st = c['st'][i]; kt = KB(tmp); ks = KB(st)
                P.op('act', lambda e: e.activation(tmp[:, 0:W], PS[pss[0]][:, 0:W], AF.Silu), reads=['ps%d' % pss[0]], writes=[kt])
                P.op('dve', lambda e: e.tensor_tensor(st[:, 0:W], tmp[:, 0:W], PS[pss[1]][:, 0:W], ALU.mult),
                     reads=[kt, 'ps%d' % pss[1]], writes=[ks])
                dma(HT[gi * 11 + u, :, t0:t0 + W], st[:, 0:W], reads=[ks])
            gemm_fm(XTsrc, 8, [(11, [(w_i, g * 1408, 1.0, 'plain'), (w_i, DFF + g * 1408, 1.0, 'plain')]) for g in range(2)],
                    tiles, ev1, extra_alloc=alloc1, slot=0, preloaded=pre_in, next_w=(KF, 8, [(w_o_, 0, 1.0, 'plain')]))
            def alloc2():
                c = {'res': [AR.f32(4096), AR.f32(4096)], 'st': [AR.f32(512) for _ in range(3)], 'n': [0], 'k': {}}
                if final:
                    c['h3'] = AR.f32(4096); c['sq'] = AR.stgall; c['rs'] = AR.f32(512); c['tp'] = c['st']
                else:
                    c['sq'] = AR.stgall; c['rs'] = AR.f32(512); c['xm'] = AR.bf16(4096)
                return c
            def pre2(ti, t0, W, gi, c):
                r = c['res'][ti % 2].rearrange("p (k t) -> p k t", k=8); k = KB(c['res'][ti % 2]); c['k'][ti] = k
                dma(r[:, :, 0:W], RES[:, :, t0:t0 + W].rearrange("k p t -> p k t"), writes=[k])
            def ev2(ti, t0, W, gi, u, pss, c):
                w = 1 if t0 >= T else 0
                r = c['res'][ti % 2].rearrange("p (k t) -> p k t", k=8)
                if final:
                    h3 = c['h3'].rearrange("p (k t) -> p k t", k=8)
                    P.op('dve', lambda e: e.scalar_tensor_tensor(h3[:, u, 0:W], PS[pss[0]][:, 0:W], DERv(w, gi_, u), r[:, u, 0:W], ALU.mult, ALU.add),
                         reads=['ps%d' % pss[0], c['k'][ti], 'DER'], writes=[KB(c['h3'])])
                else:
                    P.op('dve', lambda e: e.scalar_tensor_tensor(r[:, u, 0:W], PS[pss[0]][:, 0:W], DERv(w, gi_, u), r[:, u, 0:W], ALU.mult, ALU.add),
                         reads=['ps%d' % pss[0], c['k'][ti], 'DER'], writes=[c['k'][ti]])
            def post2(ti, t0, W, gi, c):
                if not final:
                    rb = c['res'][ti % 2]; r = rb.rearrange("p (k t) -> p k t", k=8); w = 1 if t0 >= T else 0
                    dma(OUT[:, :, t0:t0 + W].rearrange("k p t -> p k t"), r[:, :, 0:W], reads=[c['k'][ti]])
                    ai, bj, dst = norm_next
                    norm_tile(rb, W, lambda kc: DERv(w, ai, kc), lambda kc: MODv(bj, kc, w), dst, t0, c['k'][ti], c['sq'], c['rs'], c['st'], c['xm'])
                    return
                h3 = c['h3'].rearrange("p (k t) -> p k t", k=8); sqv = c['sq'].rearrange("p (k t) -> p k t", k=8)
                kh = KB(c['h3']); ksq = KB(c['sq'])
                P.op('act', lambda e: e.activation(sqv, h3, AF.Square), reads=[kh], writes=[ksq])
                b = bank()
                def f(e):
                    r = None
                    for kc in range(8):
                        r = e.matmul(PS[b], ones, sqv[:, kc, :], start=(kc == 0), stop=(kc == 7))
                    return r
                P.op('pe', f, reads=[ksq, 'ones'], writes=['ps%d' % b])
                krs = KB(c['rs'])
                rstd_from(PS[b], W, c['rs'], c['tp'][0], ['ps%d' % b], krs, KB(c['tp'][0]))
                ky = KB(c['sq'])
                for kc in range(8):
                    tp = c['tp'][1 + kc % 2]; kt = KB(tp)
                    P.op('dve', lambda e, kc=kc, tp=tp: e.tensor_tensor(tp, h3[:, kc, :], c['rs'], ALU.mult), reads=[kh, krs], writes=[kt])
                    P.op('act', lambda e, kc=kc, tp=tp: e.activation(sqv[:, kc, :], tp, AF.Copy, scale=SM[:, 112 + kc:113 + kc]),
                         reads=[kt, 'SM'], writes=[ky])
                otv = c['res'][ti % 2].rearrange("p (s d) -> p s d", d=1024); ko = KB(c['res'][ti % 2])
                for s in range(4):
                    for half in range(2):
                        b2 = bank()
                        def f2(e, s=s, half=half, b2=b2):
                            r = None
                            for q in range(4):
                                kc = half * 4 + q
                                r = e.transpose(PS[b2][:, q * 128:(q + 1) * 128], sqv[:, kc, s * 128:(s + 1) * 128], ident)
                            return r
                        P.op('pe', f2, reads=[ky, 'ident'], writes=['ps%d' % b2])
                        copy_any(('act', 'dve')[half], otv[:, s, half * 512:(half + 1) * 512], PS[b2], ['ps%d' % b2], [ko])
                dma(out_d[t0:t0 + 512, :].rearrange("(s p) d -> p s d", p=128), otv, reads=[ko])
            gemm_fm(HT, KF, [(8, [(w_o_, 0, 1.0, 'plain')])], tiles, ev2, pre_tile=pre2, post_tile=post2, extra_alloc=alloc2, slot=0, preloaded=True)

        sbuf_named.update(SM=SM, MODS=MODS, DER=DER, LG=LG, KDF=KDF, KDB=KDB, GEF=GEF, GEB=GEB, C512=C512, COF=COF, COB=COB, CTXC=CTXC,
                          CC=CC)
        stage_end(0)
        norm_pass(None, XT, TILES, 0, 0, first=True, next_w=(8, 11, [(w1i, 0, 1.0, 'plain'), (w1i, DFF, 1.0, 'plain')]))
        stage_end(1)
        ffn(XT, w1i, w1o, TILES, XF, 1, H1, norm_next=(2, 3, UT))
        stage_end(3)

        def allocq():
            c = {'tab': [AR.f32(2048), AR.f32(2048)], 'tmp': [AR.f32(512) for _ in range(6)],
                 'st': [AR.bf16(512) for _ in range(6)], 'n': [0], 'k': {}, 'qdf': AR.f32(2048), 'qdb': AR.f32(2048)}
            dma(c['qdf'], QD_D[0], writes=[KB(c['qdf'])]); dma(c['qdb'], QD_D[1], writes=[KB(c['qdb'])])
            return c
        def preq(ti, t0, W, gi, c):
            tb = c['tab'][ti % 2].rearrange("p (f t) -> p f t", f=4); k = KB(c['tab'][ti % 2]); c['k'][ti] = k
            dma(tb[:, :, 0:W], ropetab[:, :, t0:t0 + W].rearrange("f p t -> p f t"), writes=[k])
        def mk_evqk(isq):
            def ev(ti, t0, W, gi, u, pss, c):
                tb = c['tab'][ti % 2].rearrange("p (f t) -> p f t", f=4)
                i = c['n'][0] % 2; c['n'][0] += 1
                t1, t2, t3 = c['tmp'][i * 3:(i + 1) * 3]; f0 = (u % 2) * 2; h = u // 2
                k1 = KB(t1); k2 = KB(t2); k3 = KB(t3)
                P.op('dve', lambda e: e.tensor_tensor(t1[:, 0:W], PS[pss[0]][:, 0:W], tb[:, f0, 0:W], ALU.mult), reads=['ps%d' % pss[0], c['k'][ti]], writes=[k1])
                P.op('dve', lambda e: e.tensor_tensor(t2[:, 0:W], PS[pss[1]][:, 0:W], tb[:, f0 + 1, 0:W], ALU.mult), reads=['ps%d' % pss[1], c['k'][ti]], writes=[k2])
                if isq:
                    P.op('dve', lambda e: e.tensor_tensor(t3[:, 0:W], t1[:, 0:W], t2[:, 0:W], ALU.add), reads=[k1, k2], writes=[k3])
                    s0, s1, s2 = c['st'][i * 3:(i + 1) * 3]
                    ka = KB(s0); kb = KB(s1); kc_ = KB(s2)
                    P.op('act', lambda e: e.activation(s0[:, 0:W], t3[:, 0:W], AF.Copy), reads=[k3], writes=[ka])
                    P.op('dve', lambda e: e.tensor_tensor(s1[:, 0:W], t3[:, 0:W], c['qdf'][:, h * 512:h * 512 + W], ALU.mult), reads=[k3, KB(c['qdf'])], writes=[kb])
                    P.op('pool', lambda e: e.tensor_tensor(s2[:, 0:W], t3[:, 0:W], c['qdb'][:, h * 512:h * 512 + W], ALU.mult), reads=[k3, KB(c['qdb'])], writes=[kc_])
                    dma(QTR[u, :, t0:t0 + W], s0[:, 0:W], reads=[ka]); dma(QTF[u, :, t0:t0 + W], s1[:, 0:W], reads=[kb])
                    dma(QTB[u, :, t0:t0 + W], s2[:, 0:W], reads=[kc_])
                else:
                    s0 = c['st'][i * 3]; ka = KB(s0)
                    P.op('pool', lambda e: e.tensor_tensor(s0[:, 0:W], t1[:, 0:W], t2[:, 0:W], ALU.add), reads=[k1, k2], writes=[ka])
                    dma(KTR[u, :, t0:t0 + W], s0[:, 0:W], reads=[ka])
            return ev
        run_q = lambda: gemm_fm(UT, 8, [(8, [(w_in, 0, 1.0, 'plain'), (w_in, 0, 1.0, 'rot')])], MAIN, mk_evqk(True), pre_tile=preq, extra_alloc=allocq, slot=1, preloaded=True)
        gemm_fm(UT, 8, [(8, [(w_in, K0, 0.0625, 'plain'), (w_in, K0, 0.0625, 'rot')])], TILES, mk_evqk(False), pre_tile=preq, extra_alloc=allocq,
                slot=0, preloaded=False, next_w=(8, 16, [(w_in, V0, 1.0, 'plain')]))
        def evv(t0, s, cb, b, st):
            k = KB(st)
            P.op(('act', 'dve')[cb % 2] if False else 'act', lambda e: e.activation(st, PS[b], AF.Copy), reads=['ps%d' % b], writes=[k])
            dma(VV[t0 + s * 128:t0 + (s + 1) * 128, cb * 512:(cb + 1) * 512], st, reads=[k])
        gemm_tm(UT, w_in, V0, TILES, evv, slot=1, preloaded=True, next_w=(8, 16, [(w_in, G0, 1.0, 'plain')]))
        rgn = [0]
        def evrg(t0, s, cb, b, st):
            k = KB(st); tmp = AR.tm_tmp[rgn[0] % 4]; rgn[0] += 1; kt = KB(tmp)
            P.op('act', lambda e: e.activation(tmp, PS[b], AF.Silu), reads=['ps%d' % b], writes=[kt])
            P.op('dve', lambda e: e.tensor_tensor(st, tmp, AR.tm_gnw[:, cb * 512:(cb + 1) * 512], ALU.mult), reads=[kt, KB(AR.tm_gnw)], writes=[k])
            dma(RG[t0 + s * 128:t0 + (s + 1) * 128, cb * 512:(cb + 1) * 512], st, reads=[k])
        run_rg = lambda: gemm_tm(UT, w_in, G0, MAIN, evrg, slot=0, preloaded=True, next_w=(8, 8, [(w_in, C0, 1.0, 'plain'), (w_in, C0 + 1024, 1.0, 'plain')]))
        def alloccv():
            return {'tmp': [AR.f32(512) for _ in range(3)], 'st': [AR.f32(512) for _ in range(3)], 'sb': [AR.bf16(512) for _ in range(3)], 'n': [0]}
        def evcv(ti, t0, W, gi, u, pss, c):
            i = c['n'][0] % 3; c['n'][0] += 1
            tmp = c['tmp'][i]; st = c['sb'][i]; kt = KB(tmp); ks = KB(st)
            P.op('act', lambda e: e.activation(tmp, PS[pss[1]], AF.Sigmoid), reads=['ps%d' % pss[1]], writes=[kt])
            P.op('dve', lambda e: e.tensor_tensor(st, tmp, PS[pss[0]], ALU.mult), reads=[kt, 'ps%d' % pss[0]], writes=[ks])
            dma(YT[u, :, 16 + t0:16 + t0 + W], st, reads=[ks])
        run_cv = lambda: gemm_fm(UT, 8, [(8, [(w_in, C0, 1.0, 'plain'), (w_in, C0 + 1024, 1.0, 'plain')])], MAIN, evcv, extra_alloc=alloccv,
                                 slot=1, preloaded=True, next_w=(8, 16, [(w_in, GA0, 1.0, 'plain')]))
        def evg(ti, t0, W, gi, u, pss, c):
            i = c['n'][0] % 3; c['n'][0] += 1
            st = c['st'][i]; ks = KB(st)
            P.op('act', lambda e: e.activation(st, PS[pss[0]], AF.Sigmoid), reads=['ps%d' % pss[0]], writes=[ks])
            dma((SGA if u < 8 else SGB)[u % 8, :, t0:t0 + W], st, reads=[ks])
        run_gates = lambda: gemm_fm(UT, 8, [(16, [(w_in, GA0, 1.0, 'plain')])], MAIN, evg, extra_alloc=alloccv,
                                    slot=0, preloaded=True, next_w=(8, 8, [(w_in, 0, 1.0, 'plain'), (w_in, 0, 1.0, 'rot')]))

        stage_end(4)
        P.barrier(); AR.reset(); set_limit({0})
        Ef = AR.f32(4096); Eb = AR.f32(4096)
        ktile = AR.bf16(4096); vtile = AR.bf16(8192); khf = AR.bf16(4096); khb = AR.bf16(4096)
        ust = [AR.f32(512) for _ in range(4)]
        P.op('pool', lambda e: e.memset(Ef, 0.0), writes=['Ef']); P.op('pool', lambda e: e.memset(Eb, 0.0), writes=['Eb'])
        ktv = ktile.rearrange("p (k t) -> p k t", k=8); vtv = vtile.rearrange("p (s d) -> p s d", d=2048)
        khv = [khf.rearrange("p (s d) -> p s d", d=1024), khb.rearrange("p (s d) -> p s d", d=1024)]
        KDv = [KDF.rearrange("p (s h) -> p s h", h=4), KDB.rearrange("p (s h) -> p s h", h=4)]
        GEv = [GEF.rearrange("p (s h) -> p s h", h=4), GEB.rearrange("p (s h) -> p s h", h=4)]
        Ev = [Ef.rearrange("p (g e) -> p g e", e=512), Eb.rearrange("p (g e) -> p g e", e=512)]
        un = 0
        for ti, (t0, W) in enumerate(TILES):
            ns = W // 128; isctx = t0 >= T
            kk = KB(ktile); kv = KB(vtile)
            dma(ktv[:, :, 0:W], KTR[:, :, t0:t0 + W].rearrange("k p t -> p k t"), writes=[kk])
            dma(vtv[:, 0:ns, :], VV[t0:t0 + W, :].rearrange("(s p) d -> p s d", p=128), writes=[kv])
            kkh = [KB(khf), KB(khb)]
            for s in range(ns):
                b = bank(); pb = PS[b].bitcast(BF16)
                def f(e, s=s, pb=pb):
                    r = None
                    for j in range(8):
                        r = e.transpose(pb[:, j * 128:(j + 1) * 128], ktv[:, j, s * 128:(s + 1) * 128], identb)
                    return r
                P.op('pe', f, reads=[kk, 'identb'], writes=['ps%d' % b])
                for d in range(2):
                    slot = (s + 2) if (isctx and d == 0) else s
                    pbv = pb[:, 0:1024].rearrange("p (h c) -> p h c", h=4)
                    o = khv[d][:, s, :].rearrange("p (h c) -> p h c", h=4)
                    sc = KDv[d][:, slot, :].unsqueeze(2).broadcast_to([128, 4, 256])
                    P.op('dve', lambda e, o=o, pbv=pbv, sc=sc: e.tensor_tensor(o, pbv, sc, ALU.mult), reads=['ps%d' % b, 'KD'], writes=[kkh[d]])
            for d in range(2):
                for hd in range(8):
                    h = hd // 2
                    b = bank()
                    def f(e, d=d, hd=hd, h=h, b=b):
                        r = None
                        for s in range(ns):
                            r = e.matmul(PS[b], khv[d][:, s, hd * 128:(hd + 1) * 128], vtv[:, s, h * 512:(h + 1) * 512],
                                         start=(s == 0), stop=(s == ns - 1))
                        return r
                    P.op('pe', f, reads=[kkh[d], kv], writes=['ps%d' % b])
                    st = ust[un % 4]; un += 1; ks = KB(st)
                    P.op('act', lambda e, st=st, b=b: e.activation(st, PS[b], AF.Copy), reads=['ps%d' % b], writes=[ks])
                    if isctx:
                        dma(RCX[d, hd], st, reads=[ks])
                    else:
                        dma(UU[ti, d, hd], st, reads=[ks])
                        ek = 'Ef' if d == 0 else 'Eb'
                        P.op('dve', lambda e, d=d, hd=hd, h=h, b=b, ti=ti: e.scalar_tensor_tensor(Ev[d][:, hd, :], PS[b], GEv[d][:, ti, h:h + 1], Ev[d][:, hd, :], ALU.mult, ALU.add),
                             reads=['ps%d' % b, 'GE', ek], writes=[ek])
        for d in range(2):
            for h in range(4):
                dma(CINS[d * 4 + h].rearrange("(g p) e -> p g e", p=128), Ev[d][:, 2 * h:2 * h + 2, :], reads=['Ef' if d == 0 else 'Eb'], writes=['CIN'])
        stage_end(5)
        def allgather(ci_, co_, kin, kout):
            P.op('pool', lambda e: e.collective_compute("AllGather", ALU.bypass, replica_groups=[[0, 1, 2, 3], [4, 5, 6, 7]],
                                                        ins=[ci_.opt()], outs=[co_.opt()]), reads=[kin], writes=[kout], cc=True)
        for ci_, co_ in zip(CINS, COUTS): allgather(ci_, co_, 'CIN', 'COUT')
        run_rg(); run_cv()
        P.barrier()
        dma(CINH[0:1024, 0:15].rearrange("(k p) c -> k p c", p=128), YT[:, :, 16:31], writes=['CINH'])
        dma(CINH[1024:2048, 0:15].rearrange("(k p) c -> k p c", p=128), YT[:, :, 16 + T - 15:16 + T], writes=['CINH'])
        allgather(CINH, COUTH, 'CINH', 'COUTH')
        run_gates(); run_q()
        stage_end(6)
        P.barrier(); AR.reset(); set_limit(set())
        cur = [AR.f32(4096), AR.f32(4096)]; gls = [[AR.f32(4096), AR.f32(4096)], [AR.f32(4096), AR.f32(4096)]]; sbf = [[AR.bf16(4096), AR.bf16(4096)] for _ in range(2)]
        hg = AR.bf16(4 * 2 * 8 * 16); pads = AR.bf16(2 * 8 * 16)
        COv = [COF.rearrange("p (r h) -> p r h", h=4), COB.rearrange("p (r h) -> p r h", h=4)]
        G3 = lambda a: a.rearrange("p (g e) -> p g e", e=512)
        gcnt = [0, 0]
        def nextg(d):
            g = gls[d][gcnt[d] % 2]; gcnt[d] += 1
            return g
        cvs = [G3(cur[0]), G3(cur[1])]; cks = ['cur0', 'cur1']
        for d in range(2):
            g0 = nextg(d); kg = KB(g0)
            dma(G3(g0), RCX[d].rearrange("g p e -> p g e"), writes=[kg])
            for h in range(4):
                P.op('dve', lambda e, g0=g0, h=h, d=d: e.tensor_scalar(cvs[d][:, 2 * h:2 * h + 2, :], G3(g0)[:, 2 * h:2 * h + 2, :],
                                                                      CTXC[:, 4 * d + h:4 * d + h + 1], None, ALU.mult), reads=[kg, 'CTXC'], writes=[cks[d]])
        def ld_rank(r, d):
            g1 = nextg(d)
            for h in range(4):
                dma(G3(g1)[:, 2 * h:2 * h + 2, :], COUTS[d * 4 + h][r * 256:(r + 1) * 256, :].rearrange("(g p) e -> p g e", p=128), reads=['COUT'], writes=[KB(g1)])
            return g1
        def ld_u(i, d):
            g1 = nextg(d)
            dma(G3(g1), UU[i, d].rearrange("g p e -> p g e"), writes=[KB(g1)])
            return g1
        tidx = lambda n_, d: n_ if d == 0 else 7 - n_
        pend = {d: ld_rank(0, d) for d in range(2)}
        for r in range(4):
            for d in range(2):
                g1 = pend[d]; kg = KB(g1)
                pend[d] = ld_rank(r + 1, d) if r < 3 else ld_u(tidx(0, d), d)
                for h in range(4):
                    P.op('dve', lambda e, g1=g1, h=h, d=d, r=r: e.scalar_tensor_tensor(cvs[d][:, 2 * h:2 * h + 2, :], G3(g1)[:, 2 * h:2 * h + 2, :],
                                                                                     COv[d][:, r, h:h + 1], cvs[d][:, 2 * h:2 * h + 2, :], ALU.mult, ALU.add),
                         reads=[kg, 'COF', 'COB', cks[d]], writes=[cks[d]])
        for n_ in range(8):
            for d in range(2):
                i = tidx(n_, d)
                DST = SF if d == 0 else SB
                sb_ = sbf[d][n_ % 2]; kb = KB(sb_)
                P.op('act', lambda e, sb_=sb_, d=d: e.activation(sb_, cur[d], AF.Copy), reads=[cks[d]], writes=[kb])
                if n_ < 7:
                    g1 = pend[d]; kg = KB(g1)
                    if n_ < 6: pend[d] = ld_u(tidx(n_ + 1, d), d)
                    for h in range(4):
                        P.op('dve', lambda e, g1=g1, h=h, d=d: e.scalar_tensor_tensor(cvs[d][:, 2 * h:2 * h + 2, :], cvs[d][:, 2 * h:2 * h + 2, :], C512[:, 4 * d + h:4 * d + h + 1],
                                                                                    G3(g1)[:, 2 * h:2 * h + 2, :], ALU.mult, ALU.add),
                             reads=[kg, 'C512', cks[d]], writes=[cks[d]])
                dma(DST[i].rearrange("g p e -> p g e"), G3(sb_), reads=[kb])
        hgv = hg.rearrange("p (r s k c) -> p r s k c", r=4, s=2, k=8)
        for r in range(4):
            for s in range(2):
                dma(hgv[:, r, s, :, :], COUTH[r * 2048 + s * 1024:r * 2048 + (s + 1) * 1024, :].rearrange("(k p) c -> p k c", p=128), reads=['COUTH'], writes=['hg'])
        pv = pads.rearrange("p (s k c) -> p s k c", s=2, k=8)
        for side in range(2):
            for r in range(4):
                mcol = (18 if side == 0 else 22) + r
                if r == 0:
                    P.op('dve', lambda e, side=side, r=r, mcol=mcol: e.tensor_scalar(pv[:, side], hgv[:, r, 1 - side], CC[:, mcol:mcol + 1], None, ALU.mult), reads=['hg', 'CC'], writes=['pads'])
                else:
                    P.op('dve', lambda e, side=side, r=r, mcol=mcol: e.scalar_tensor_tensor(pv[:, side], hgv[:, r, 1 - side], CC[:, mcol:mcol + 1], pv[:, side], ALU.mult, ALU.add),
                         reads=['hg', 'CC', 'pads'], writes=['pads'])
        dma(YT[:, :, 1:16].rearrange("k p c -> p k c"), pv[:, 0, :, 0:15], reads=['pads'])
        dma(YT[:, :, 16 + T:16 + T + 15].rearrange("k p c -> p k c"), pv[:, 1, :, 0:15], reads=['pads'])

        stage_end(7)
        P.barrier(); AR.reset(); set_limit(set())
        qkb = [[AR.bf16(4096) for _ in range(4)] for _ in range(2)]; vt = AR.bf16(8192); rgt = AR.bf16(8192)
        sft = AR.bf16(4096); sbt = AR.bf16(4096); pts = [AR.bf16(2048), AR.bf16(2048)]; zz = AR.bf16(8192); zts = AR.bf16(4096)
        junk = AR.bf16(512); og = [AR.f32(512), AR.f32(512)]; stt = AR.f32(16 * 8)
        DMT = AR.f32(8192)
        dma(DMT, DMT_D, writes=['DMT'])
        V3 = lambda a, k: a.rearrange("p (k t) -> p k t", k=k)
        sfv, sbv = V3(sft, 8), V3(sbt, 8)
        vtv = vt.rearrange("p (s d) -> p s d", d=2048); rgv = rgt.rearrange("p (s d) -> p s d", d=2048)
        ptvs = [p_.rearrange("p (s c) -> p s c", c=512) for p_ in pts]; zv = zz.rearrange("p (s d) -> p s d", d=2048); ztv = V3(zts, 8)
        def ld_qk(ti):
            t0 = MAIN[ti][0]
            for n__, src in enumerate((QTR, QTF, QTB, KTR)):
                dma(V3(qkb[ti % 2][n__], 8), src[:, :, t0:t0 + 512].rearrange("k p t -> p k t"), writes=[KB(qkb[ti % 2][n__])])
        def ld_head(ti, h):
            t0 = MAIN[ti][0]
            dma(vtv[:, :, h * 512:(h + 1) * 512], VV[t0:t0 + 512, h * 512:(h + 1) * 512].rearrange("(s p) d -> p s d", p=128), writes=['vt_h%d' % h])
            dma(sfv[:, 2 * h:2 * h + 2, :], SF[ti][2 * h:2 * h + 2].rearrange("g p e -> p g e"), writes=['sf_h%d' % h])
            dma(sbv[:, 2 * h:2 * h + 2, :], SB[ti][2 * h:2 * h + 2].rearrange("g p e -> p g e"), writes=['sb_h%d' % h])
            dma(rgv[:, :, h * 512:(h + 1) * 512], RG[t0:t0 + 512, h * 512:(h + 1) * 512].rearrange("(s p) d -> p s d", p=128), writes=['rg_h%d' % h])
        ld_qk(0)
        for ti, (t0, W) in enumerate(MAIN):
            qr, qf, qb, kt = qkb[ti % 2]
            qrv, qfv, qbv, ktv = V3(qr, 8), V3(qf, 8), V3(qb, 8), V3(kt, 8)
            if ti + 1 < 8: ld_qk(ti + 1)
            if ti == 0:
                for h in range(4): ld_head(0, h)
            kz = KB(zz)
            klq = [KB(a_) for a_ in (qr, qf, qb, kt)]
            def emit_S(h):
                ptv = ptvs[h % 2]; kp = KB(pts[h % 2])
                for sc in range(4):
                    b = bank()
                    def f(e, sc=sc, b=b):
                        r = None
                        for dc in range(2):
                            r = e.matmul(PS[b], ktv[:, 2 * h + dc, sc * 128:(sc + 1) * 128], qrv[:, 2 * h + dc, :], start=(dc == 0), stop=(dc == 1))
                        return r
                    P.op('pe', f, reads=[klq], writes=['ps%d' % b])
                    P.op('dve', lambda e, sc=sc, b=b: e.tensor_tensor(ptv[:, sc, :], PS[b], DMT[:, h * 2048 + sc * 512:h * 2048 + (sc + 1) * 512], ALU.mult),
                         reads=['ps%d' % b, 'DMT'], writes=[kp])
            emit_S(0)
            for h in range(4):
                kl = klq + ['vt_h%d' % h, 'sf_h%d' % h, 'sb_h%d' % h]
                krg = 'rg_h%d' % h
                ptv = ptvs[h % 2]; kp = KB(pts[h % 2])
                if h + 1 < 4: emit_S(h + 1)
                for s in range(4):
                    b = bank(); idx = h * 4 + s
                    def f(e, h=h, s=s, b=b):
                        for sc in range(4):
                            e.matmul(PS[b], ptv[:, sc, s * 128:(s + 1) * 128], vtv[:, sc, h * 512:(h + 1) * 512], start=(sc == 0), stop=False)
                        r = None
                        for dc in range(2):
                            e.matmul(PS[b], qfv[:, 2 * h + dc, s * 128:(s + 1) * 128], sfv[:, 2 * h + dc, :], start=False, stop=False)
                        for dc in range(2):
                            r = e.matmul(PS[b], qbv[:, 2 * h + dc, s * 128:(s + 1) * 128], sbv[:, 2 * h + dc, :], start=False, stop=(dc == 1))
                        return r
                    P.op('pe', f, reads=[kl, kp], writes=['ps%d' % b])
                    sv = lambda c, idx=idx: stt[:, idx * 8 + c:idx * 8 + c + 1]
                    k1, k2, k3, k4, k5, k6, k7, k8 = ['stt%d_%d' % (idx, c_) for c_ in range(8)]
                    pk = 'ps%d' % b
                    P.op('act', lambda e, b=b, sv=sv: e.activation(junk, PS[b], AF.Copy, accum_out=sv(0)), reads=[pk], writes=[k1, 'junk'])
                    P.op('act', lambda e, b=b, sv=sv: e.activation(junk, PS[b], AF.Square, accum_out=sv(1)), reads=[pk], writes=[k2, 'junk'])
                    P.op('dve', lambda e, sv=sv: e.tensor_scalar(sv(2), sv(0), -1.0 / 512.0, None, ALU.mult), reads=[k1], writes=[k3])
                    P.op('dve', lambda e, sv=sv: e.tensor_tensor(sv(3), sv(2), sv(2), ALU.mult), reads=[k3], writes=[k4])
                    P.op('dve', lambda e, sv=sv: e.scalar_tensor_tensor(sv(4), sv(1), 1.0 / 512.0, sv(3), ALU.mult, ALU.subtract), reads=[k2, k4], writes=[k5])
                    P.op('act', lambda e, sv=sv: e.activation(sv(5), sv(4), AF.Ln, bias=EPS, scale=1.0), reads=[k5], writes=[k6])
                    P.op('act', lambda e, sv=sv: e.activation(sv(6), sv(5), AF.Exp, scale=-0.5), reads=[k6], writes=[k7])
                    o1 = og[idx % 2]; ko = KB(o1)
                    P.op('dve', lambda e, b=b, sv=sv, o1=o1: e.tensor_scalar(o1, PS[b], sv(2), sv(6), ALU.add, ALU.mult), reads=[pk, k3, k7], writes=[ko])
                    P.op('pool', lambda e, o1=o1, h=h, s=s: e.tensor_tensor(zv[:, s, h * 512:(h + 1) * 512], o1, rgv[:, s, h * 512:(h + 1) * 512], ALU.mult),
                         reads=[ko, krg], writes=[kz])
                if ti + 1 < 8: ld_head(ti + 1, h)
            kzt = KB(zts)
            for half in range(2):
                for s in range(4):
                    b = bank(); pb = PS[b].bitcast(BF16)
                    def f(e, s=s, half=half, pb=pb):
                        r = None
                        for q in range(8):
                            j = half * 8 + q
                            r = e.transpose(pb[:, q * 128:(q + 1) * 128], zv[:, s, j * 128:(j + 1) * 128], identb)
                        return r
                    P.op('pe', f, reads=[kz, 'identb'], writes=['ps%d' % b])
                    copy_any(('act', 'dve')[s % 2], ztv[:, :, s * 128:(s + 1) * 128],
                             pb[:, 0:1024].rearrange("p (q c) -> p q c", q=8), ['ps%d' % b], [kzt])
                dma(ZT[half * 8:(half + 1) * 8, :, t0:t0 + 512].rearrange("k p t -> p k t"), ztv, reads=[kzt])

        stage_end(8)
        def allocA(names):
            def a():
                c = {'st': [AR.f32(512) for _ in range(3)], 'sb': [AR.bf16(512) for _ in range(3)], 'tmp': [AR.f32(512) for _ in range(3)], 'n': [0], 'k': {}}
                for nme in names: c[nme] = [AR.f32(4096), AR.f32(4096)]
                return c
            return a
        def mkpre(pairs):
            def pre(ti, t0, W, gi, c):
                for nme, src in pairs:
                    r = c[nme][ti % 2].rearrange("p (k t) -> p k t", k=8); k = KB(c[nme][ti % 2]); c['k'][(nme, ti)] = k
                    dma(r, src[:, :, t0:t0 + 512].rearrange("k p t -> p k t"), writes=[k])
            return pre
        def ev_ret(ti, t0, W, gi, u, pss, c):
            i = c['n'][0] % 3; c['n'][0] += 1
            st = c['st'][i]; ks = KB(st); r = c['sga'][ti % 2].rearrange("p (k t) -> p k t", k=8)
            P.op('dve', lambda e: e.tensor_tensor(st, PS[pss[0]], r[:, u, :], ALU.mult), reads=['ps%d' % pss[0], c['k'][('sga', ti)]], writes=[ks])
            dma(M1[u, :, t0:t0 + 512], st, reads=[ks])
        gemm_fm(ZT, 16, [(8, [(w_ro, 0, 1.0, 'plain')])], MAIN, ev_ret, pre_tile=mkpre([('sga', SGA)]), extra_alloc=allocA(['sga']),
                slot=1, preloaded=False, next_w=(8, 8, [(w_co, 0, 1.0, 'plain')]))

        P.barrier(); AR.reset(); set_limit({0})
        DG = AR.bf16(248 * 128)
        ypb = [AR.bf16(8 * 544), AR.bf16(8 * 544)]; yc = AR.f32(4096); sqs = [AR.f32(512), AR.f32(512)]; mean = AR.f32(512); m2 = AR.f32(512)
        var = AR.f32(512); rs = AR.f32(512); tl = AR.f32(512); tps = [AR.f32(512) for _ in range(4)]; ct = AR.bf16(4096)
        ycv = V3(yc, 8)
        CW = lambda k, kc: SM[:, 144 + k * 8 + kc:145 + k * 8 + kc]
        kdg = KB(DG)
        for q4 in range(4):
            dma(DG[:, q4 * 7936:(q4 + 1) * 7936], DG_D[:, q4 * 7936:(q4 + 1) * 7936], writes=[kdg])
        def ld_y(ti):
            t0 = MAIN[ti][0]
            dma(ypb[ti % 2].rearrange("p (k t) -> p k t", k=8)[:, :, 0:542], YT[:, :, t0 + 1:t0 + 543].rearrange("k p t -> p k t"), writes=[KB(ypb[ti % 2])])
        ld_y(0)
        for ti, (t0, W) in enumerate(MAIN):
            yp = ypb[ti % 2].rearrange("p (k t) -> p k t", k=8); ky = KB(ypb[ti % 2])
            if ti + 1 < 8: ld_y(ti + 1)
            kyc = KB(yc)
            b1 = bank(); b2 = bank()
            for kc in range(8):
                b = bank()
                while b in (b1, b2): b = bank()
                def f(e, kc=kc, yp=yp, b=b):
                    for k in range(31):
                        j = k * 8 + kc
                        e.matmul(PS[b], DG[:, j * 128:(j + 1) * 128], yp[:, kc, k:k + 512], start=(k == 0), stop=(k == 30))
                P.op('pe', f, reads=[kdg, ky], writes=['ps%d' % b])
                P.op('act', lambda e, kc=kc, b=b: e.activation(ycv[:, kc, :], PS[b], AF.Identity, bias=SM[:, 120 + kc:121 + kc], scale=1.0),
                     reads=['ps%d' % b, 'SM'], writes=[kyc])
                sq_ = sqs[kc % 2]; ksq = KB(sq_)
                P.op('act', lambda e, kc=kc, sq_=sq_: e.activation(sq_, ycv[:, kc, :], AF.Square), reads=[kyc], writes=[ksq])
                P.op('pe', lambda e, kc=kc: e.matmul(PS[b1], ones, ycv[:, kc, :], start=(kc == 0), stop=(kc == 7)), reads=[kyc, 'ones'], writes=['ps%d' % b1])
                P.op('pe', lambda e, kc=kc, sq_=sq_: e.matmul(PS[b2], ones, sq_, start=(kc == 0), stop=(kc == 7)), reads=[ksq, 'ones'], writes=['ps%d' % b2])
            km = KB(mean); km2 = KB(m2); kv = KB(var); krs = KB(rs)
            P.op('act', lambda e, b1=b1: e.activation(mean, PS[b1], AF.Copy), reads=['ps%d' % b1], writes=[km])
            P.op('dve', lambda e: e.tensor_tensor(m2, mean, mean, ALU.mult), reads=[km], writes=[km2])
            P.op('dve', lambda e, b2=b2: e.tensor_tensor(var, PS[b2], m2, ALU.subtract), reads=['ps%d' % b2, km2], writes=[kv])
            rstd_from(var, 512, rs, tl, [kv], krs, KB(tl))
            ctv = V3(ct, 8); kct = KB(ct)
            for kc in range(8):
                ta = tps[(kc % 2) * 2]; tb_ = tps[(kc % 2) * 2 + 1]; ka = KB(ta); kb = KB(tb_); kc2 = KB(ta)
                P.op('dve', lambda e, kc=kc, ta=ta: e.tensor_tensor(ta, ycv[:, kc, :], mean, ALU.subtract), reads=[kyc, km], writes=[ka])
                P.op('dve', lambda e, ta=ta, tb_=tb_: e.tensor_tensor(tb_, ta, rs, ALU.mult), reads=[ka, krs], writes=[kb])
                P.op('dve', lambda e, kc=kc, ta=ta, tb_=tb_: e.tensor_scalar(ta, tb_, SM[:, 128 + kc:129 + kc], SM[:, 136 + kc:137 + kc], ALU.mult, ALU.add), reads=[kb, 'SM'], writes=[kc2])
                P.op('act', lambda e, kc=kc, ta=ta: e.activation(ctv[:, kc, :], ta, AF.Silu), reads=[kc2], writes=[kct])
            dma(CT[:, :, t0:t0 + 512].rearrange("k p t -> p k t"), ctv, reads=[kct])

        def ev_conv(ti, t0, W, gi, u, pss, c):
            i = c['n'][0] % 3; c['n'][0] += 1
            tmp = c['tmp'][i]; sb_ = c['sb'][i]; kt_ = KB(tmp); ks = KB(sb_)
            r = c['sgb'][ti % 2].rearrange("p (k t) -> p k t", k=8); m = c['m1'][ti % 2].rearrange("p (k t) -> p k t", k=8)
            P.op('dve', lambda e: e.tensor_tensor(tmp, PS[pss[0]], r[:, u, :], ALU.mult), reads=['ps%d' % pss[0], c['k'][('sgb', ti)]], writes=[kt_])
            P.op('pool', lambda e: e.tensor_tensor(sb_, tmp, m[:, u, :], ALU.add), reads=[kt_, c['k'][('m1', ti)]], writes=[ks])
            dma(MT[u, :, t0:t0 + 512], sb_, reads=[ks])
        gemm_fm(CT, 8, [(8, [(w_co, 0, 1.0, 'plain')])], MAIN, ev_conv, pre_tile=mkpre([('sgb', SGB), ('m1', M1)]), extra_alloc=allocA(['sgb', 'm1']),
                slot=0, preloaded=True)
        def ev_out(ti, t0, W, gi, u, pss, c):
            r = c['h1'][ti % 2].rearrange("p (k t) -> p k t", k=8); k = c['k'][('h1', ti)]
            P.op('dve', lambda e: e.scalar_tensor_tensor(r[:, u, :], PS[pss[0]], MODv(5, u, 0), r[:, u, :], ALU.mult, ALU.add),
                 reads=['ps%d' % pss[0], k, 'MODS'], writes=[k])
        def alloc_out():
            c = {'st': [AR.f32(512) for _ in range(3)], 'n': [0], 'k': {}, 'h1': [AR.f32(4096), AR.f32(4096)]}
            c['sq'] = AR.f32(4096); c['rs'] = AR.f32(512); c['xm'] = AR.bf16(4096)
            return c
        def post_out(ti, t0, W, gi, c):
            rb = c['h1'][ti % 2]; r = rb.rearrange("p (k t) -> p k t", k=8); k = c['k'][('h1', ti)]
            dma(H2[:, :, t0:t0 + 512].rearrange("k p t -> p k t"), r, reads=[k])
            norm_tile(rb, 512, lambda kc: DERv(0, 3, kc), lambda kc: MODv(6, kc, 0), XT, t0, k, c['sq'], c['rs'], c['st'], c['xm'])
        gemm_fm(MT, 8, [(8, [(w_o, 0, 1.0, 'plain')])], MAIN, ev_out, pre_tile=mkpre([('h1', H1)]), post_tile=post_out, extra_alloc=alloc_out,
                slot=1, preloaded=False, next_w=(8, 11, [(w2i, 0, 1.0, 'plain'), (w2i, DFF, 1.0, 'plain')]))

        stage_end(9)
        ffn(XT, w2i, w2o, MAIN, H2, 4, None, final=True)
        P.emit(nc)
    return nc


def _host_inputs(x, c, ctx, c_ctx, w_mod, b_mod, norm_ffn1, w_ffn1_in, w_ffn1_out, norm_mix, w_in,
                 ret_decay_f, ret_decay_b, ret_gn_w, w_ret_out, conv_w, conv_b, conv_ln_w, conv_ln_b,
                 w_conv_out, w_out, norm_ffn2, w_ffn2_in, w_ffn2_out, final_norm):
    f = lambda a: np.ascontiguousarray(np.asarray(a, dtype=np.float32))
    shared = dict(w_mod=f(w_mod[0]), w_ffn1_in=f(w_ffn1_in[0]), w_ffn1_out=f(w_ffn1_out[0]), w_in=f(w_in[0]),
                  w_ret_out=f(w_ret_out[0]), w_conv_out=f(w_conv_out[0]), w_out=f(w_out[0]),
                  w_ffn2_in=f(w_ffn2_in[0]), w_ffn2_out=f(w_ffn2_out[0]), gnw=f(ret_gn_w[0]).reshape(1, 2048),
                  decay=np.concatenate([f(ret_decay_f[0]), f(ret_decay_b[0])]).reshape(1, 8),
                  ident=np.eye(128, dtype=np.float32))
    p = np.arange(128, dtype=np.float32)[:, None]
    pc = np.zeros((128, NPC), np.float32)
    sc = np.arange(4, dtype=np.float32)[None, :]
    pc[:, 0:4] = 511 - 128 * sc - p; pc[:, 4:8] = 128 * sc + p
    i8 = np.arange(8, dtype=np.float32)[None, :]
    pc[:, 8:16] = 512 * (7 - i8); pc[:, 16:24] = 512 * i8
    cc = np.arange(512, dtype=np.float32)[None, :]
    pc[:, 24:536] = cc + 1; pc[:, 536:1048] = 512 - cc
    for s in range(4):
        pc[:, 1048 + s * 512:1048 + (s + 1) * 512] = cc - 128 * s - p
    shared['pconst'] = pc
    half = 128
    inv = (10000.0 ** (-np.arange(0, half, 2, dtype=np.float32) / half)).astype(np.float32)
    invp = np.concatenate([inv, inv])[:, None]
    maps = []
    for core in range(8):
        b = core // 4; s = core % 4
        d = dict(shared)
        d['xin'] = np.ascontiguousarray(np.concatenate([f(x[b, s * T:(s + 1) * T]), f(ctx[b])], 0))
        sm = np.zeros((512, 128), np.float32)
        sm[0:72] = f(b_mod[0]).reshape(72, 128)
        for r0, v in ((72, c[b]), (80, c_ctx), (88, norm_ffn1[0]), (96, norm_mix[0]), (104, norm_ffn2[0]), (112, final_norm),
                      (120, conv_b[0]), (128, conv_ln_w[0]), (136, conv_ln_b[0])):
            sm[r0:r0 + 8] = f(v).reshape(8, 128)
        sm[144:144 + 248] = f(conv_w[0]).reshape(31 * 8, 128)
        d['smalls'] = sm
        tpos = (s * T + np.arange(T)).astype(np.float32)
        rows = np.floor(tpos / 64.0).astype(np.float32); cols = (tpos - rows * 64).astype(np.float32)
        tab = np.zeros((4, 128, TA), np.float32)
        ar = (rows[None, :] * invp).astype(np.float32); ac = (cols[None, :] * invp).astype(np.float32)
        tab[0, :, :T] = np.cos(ar); tab[1, :, :T] = np.sin(ar); tab[2, :, :T] = np.cos(ac); tab[3, :, :T] = np.sin(ac)
        tab[0, :, T:] = 1.0; tab[2, :, T:] = 1.0
        d['ropetab'] = tab
        cc_ = np.zeros((1, 32), np.float32)
        for r in range(4):
            if r < s: cc_[0, r] = T * (s - 1 - r); cc_[0, 4 + r] = 1.0
            if r > s: cc_[0, 8 + r] = T * (r - s - 1); cc_[0, 12 + r] = 1.0
            if r == s - 1: cc_[0, 18 + r] = 1.0
            if r == s + 1: cc_[0, 22 + r] = 1.0
        cc_[0, 16] = T * s; cc_[0, 17] = T * (3 - s)
        d['cconst'] = cc_
        maps.append(d)
    return maps


_NC = [None]


def kernel(**inputs):
    maps = _host_inputs(**inputs)
    if _NC[0] is None:
        _NC[0] = build_nc()
    res = run_bass_kernel_spmd(_NC[0], maps, core_ids=list(range(8)))
    out = np.zeros((2, 4 * T, DM_), np.float32)
    for core in range(8):
        out[core // 4, (core % 4) * T:(core % 4 + 1) * T] = res.results[core]["out"]
    return out
```

```python
import numpy as np
from concourse.bass_utils import run_bass_kernel_spmd
import concourse.bass as bass
import concourse.mybir as mybir

F32 = mybir.dt.float32
BF16 = mybir.dt.bfloat16
ALU = mybir.AluOpType
AF = mybir.ActivationFunctionType
ENGS = ['pe', 'act', 'dve', 'pool', 'sp']


def _flat(x):
    o = []
    for k in x:
        if isinstance(k, (list, tuple)): o.extend(_flat(k))
        else: o.append(k)
    return o


class _Rec:
    def __init__(self): self.calls = []
    def __getattr__(self, name):
        def call(*a, **kw):
            self.calls.append((name, a, kw)); return None
        return call


class Ins:
    __slots__ = ('eng', 'fn', 'deps', 'is_dma', 'sem', 'val', 'prev', 'signal', 'cc')

    def __init__(self, eng, fn, is_dma, cc=False):
        self.eng = eng; self.fn = fn; self.is_dma = is_dma; self.deps = []
        self.sem = None; self.val = 0; self.prev = 0; self.signal = False; self.cc = cc


class Prog:
    def __init__(self):
        self.streams = {e: [] for e in ENGS}
        self.bufs = {}
        self.bar = {e: None for e in ENGS}
        self.all_dma = []

    def op(self, eng, fn, reads=(), writes=(), dma=False, cc=False):
        rec_ = _Rec(); fn(rec_); calls_ = rec_.calls
        assert calls_, 'op recorded no calls'
        def fn(e, calls_=calls_):
            r = None
            for nm_, a_, kw_ in calls_:
                r = getattr(e, nm_)(*a_, **kw_)
            return r
        ins = Ins(eng, fn, dma or cc, cc)
        reads = _flat(reads); writes = _flat(writes)
        writes = writes + [k for k in reads if isinstance(k, str) and k.startswith('ps') and k not in writes]
        raw = set(); other = set()
        for k in reads:
            st = self.bufs.get(k)
            if st: raw.update(st['w'])
        for k in writes:
            st = self.bufs.get(k)
            if st:
                other.update(st['w']); other.update(st['r'])
        if self.bar[eng] is not None:
            raw.update(self.bar[eng]); self.bar[eng] = None
        deps = set(raw)
        for d in other:
            if d.eng == eng and not d.is_dma and not ins.is_dma and eng == 'pe':
                continue
            if d.eng == eng and not d.is_dma and ins.is_dma and False:
                continue
            deps.add(d)
        if eng == 'pe':
            deps = {d for d in deps if not (d.eng == 'pe' and not d.is_dma)}
        deps.discard(ins)
        ins.deps = list(deps)
        for d in ins.deps: d.signal = True
        for k in reads:
            st = self.bufs.setdefault(k, {'w': [], 'r': []})
            st['r'].append(ins)
        for k in writes:
            st = self.bufs.get(k)
            if st is None or st['r']:
                self.bufs[k] = {'w': [ins], 'r': []}
            else:
                st['w'].append(ins)
        self.streams[eng].append(ins)
        if ins.is_dma and not ins.cc: self.all_dma.append(ins)
        return ins

    def barrier(self):
        last = []
        for e in ENGS:
            s = self.streams[e]
            for ins in reversed(s):
                if not ins.is_dma:
                    last.append(ins); break
        last.extend(self.all_dma)
        self.all_dma = []
        for e in ENGS: self.bar[e] = list(last) + (self.bar[e] or [])
        self.bufs = {k: v for k, v in self.bufs.items() if isinstance(k, str) and k.startswith('COUT')}

    def emit(self, nc, npool=24):
        from contextlib import ExitStack
        with ExitStack() as es:
            esem = {e: es.enter_context(nc.semaphore('es_' + e)) for e in ENGS}
            pools = {e: [es.enter_context(nc.semaphore('dp_%s_%d' % (e, i))) for i in range(npool)]
                     for e in ('sp', 'act', 'pool')}
            ccsem = es.enter_context(nc.semaphore('ccsem'))
            finals = {}
            ccn = 0
            for e in ENGS:
                cnt = 0; rr = 0; cum = {}
                for ins in self.streams[e]:
                    if ins.cc:
                        ccn += 1; ins.sem = ccsem; ins.val = ccn; ins.prev = 0
                    elif ins.is_dma:
                        s = pools[e][rr % npool]; rr += 1
                        ins.sem = s; ins.prev = cum.get(id(s), 0)
                        cum[id(s)] = ins.prev + 16; ins.val = ins.prev + 16
                        finals[id(s)] = (s, ins.val)
                    elif ins.signal:
                        cnt += 1; ins.sem = esem[e]; ins.val = cnt
            self.maxcnt = {e: sum(1 for i in self.streams[e] if i.signal and not i.is_dma) for e in ENGS}
            block = es.enter_context(nc.Block())

            def run(e, eng):
                waited = {}
                for ins in self.streams[e]:
                    for d in ins.deps:
                        if waited.get(id(d.sem), 0) < d.val:
                            eng.wait_ge(d.sem, d.val); waited[id(d.sem)] = d.val
                    if ins.is_dma and not ins.cc and ins.prev > 0 and waited.get(id(ins.sem), 0) < ins.prev:
                        eng.wait_ge(ins.sem, ins.prev); waited[id(ins.sem)] = ins.prev
                    r = ins.fn(eng)
                    if ins.cc:
                        r.then_inc(ins.sem)
                    elif ins.is_dma:
                        r.then_inc(ins.sem, 16)
                    elif ins.signal:
                        r.then_inc(ins.sem, 1)
                if e == 'sp':
                    for s, v in finals.values():
                        if waited.get(id(s), 0) < v: eng.wait_ge(s, v)

            @block.tensor
            def _(eng): run('pe', eng)

            @block.scalar
            def _(eng): run('act', eng)

            @block.vector
            def _(eng): run('dve', eng)

            @block.gpsimd
            def _(eng): run('pool', eng)

            @block.sync
            def _(eng): run('sp', eng)


DM_ = 1024; T = 4096; TC = 256; TA = T + TC; DFF = 2816; KF = 22
TILES = [(i * 512, 512) for i in range(8)] + [(T, TC)]
MAIN = TILES[:8]
K0, V0, G0, C0, GA0, GB0 = 1024, 2048, 4096, 6144, 8192, 9216
NPC = 3096
EPS = 1e-6


class Arena:
    def __init__(self, ap, n): self.ap = ap; self.n = n; self.off = 0; self.reg = []; self.subs = {}; self.limit = n
    def reset(self): self.off = 0; self.reg = []; self.subs = {}
    def f32(self, n, _reg=True):
        assert self.off + n <= min(self.n, self.limit), ("arena overflow", self.off + n, self.limit)
        a = self.ap[:, self.off:self.off + n]
        if _reg: self.reg.append((a, 'A%d' % self.off))
        self.off += n
        return a
    def bf16(self, n):
        assert n % 2 == 0
        off = self.off
        a = self.f32(n // 2, _reg=False).bitcast(BF16)
        self.reg.append((a, 'A%d' % off))
        return a
    def f32_sub(self, base, off, n):
        v = base[:, off:off + n]
        k = self.key(base) + '_s%d' % off
        self.reg.append((v, k)); self.subs.setdefault(self.key(base), []).append(k)
        return v
    def keys_all(self, base):
        return [self.key(base)] + self.subs.get(self.key(base), [])
    def alias(self, view, base):
        self.reg.append((view, self.key(base))); return view
    def key(self, a):
        for o, k in self.reg:
            if o is a: return k
        raise KeyError("unregistered buffer")


class _Stop(Exception):
    pass


def build_nc(stage=99, dumps=()):
    try:
        return _build_nc(stage, dumps)
    except _Stop as e_:
        return e_.args[0]


def _build_nc(stage=99, dumps=()):
    from contextlib import ExitStack
    nc = bass.Bass("TRN2", target_bir_lowering=False)
    def EI(name, shape): return nc.dram_tensor(name, list(shape), F32, kind="ExternalInput").ap()
    xin = EI("xin", [TA, DM_]); smalls = EI("smalls", [512, 128]); gnw_d = EI("gnw", [1, 2048])
    decay_d = EI("decay", [1, 8]); ropetab = EI("ropetab", [4, 128, TA]); pconst = EI("pconst", [128, NPC])
    ident_d = EI("ident", [128, 128]); cconst = EI("cconst", [1, 32])
    w_mod = EI("w_mod", [DM_, 9216]); w1i = EI("w_ffn1_in", [DM_, 2 * DFF]); w1o = EI("w_ffn1_out", [DFF, DM_])
    w_in = EI("w_in", [DM_, 10240]); w_ro = EI("w_ret_out", [2048, DM_]); w_co = EI("w_conv_out", [DM_, DM_])
    w_o = EI("w_out", [DM_, DM_]); w2i = EI("w_ffn2_in", [DM_, 2 * DFF]); w2o = EI("w_ffn2_out", [DFF, DM_])
    out_d = nc.dram_tensor("out", [T, DM_], F32, kind="ExternalOutput").ap()
    def DT(name, shape, dt): return nc.dram_tensor(name, list(shape), dt).ap()
    XF = DT("XF", [8, 128, TA], F32); XT = DT("XT", [8, 128, TA], BF16); HT = DT("HT", [KF, 128, TA], BF16)
    H1 = DT("H1", [8, 128, TA], F32); UT = DT("UT", [8, 128, TA], BF16)
    QTR = DT("QTR", [8, 128, T], BF16); QTF = DT("QTF", [8, 128, T], BF16); QTB = DT("QTB", [8, 128, T], BF16)
    KTR = DT("KTR", [8, 128, TA], BF16); VV = DT("VV", [TA, 2048], BF16); RG = DT("RG", [T, 2048], BF16)
    YT = DT("YT", [8, 128, T + 32], BF16); SGA = DT("SGA", [8, 128, T], F32); SGB = DT("SGB", [8, 128, T], F32)
    UU = DT("UU", [8, 2, 8, 128, 512], F32); SF = DT("SF", [8, 8, 128, 512], BF16); SB = DT("SB", [8, 8, 128, 512], BF16)
    ZT = DT("ZT", [16, 128, T], BF16); M1 = DT("M1", [8, 128, T], F32); CT = DT("CT", [8, 128, T], BF16)
    MT = DT("MT", [8, 128, T], BF16); H2 = DT("H2", [8, 128, T], F32)
    CINS = [DT("CIN%d" % i, [256, 512], F32) for i in range(8)]; COUTS = [DT("COUT%d" % i, [4 * 256, 512], F32) for i in range(8)]
    CINH = DT("CINH", [2048, 16], BF16); COUTH = DT("COUTH", [4 * 2048, 16], BF16)
    RCX = DT("RCX", [2, 8, 128, 512], F32)
    QD_D = DT("QD_D", [2, 128, 2048], F32); DMT_D = DT("DMT_D", [128, 8192], F32); DG_D = DT("DG_D", [128, 248 * 128], BF16)
    P = Prog()
    es = ExitStack()
    with es:
        def SBT(name, n, dt=F32): return es.enter_context(nc.sbuf_tensor("s_" + name, [128, n], dt))[:, :]
        ident = SBT("ident", 128); identb = SBT("identb", 128, BF16); ones = SBT("ones", 128)
        SM = SBT("SM", 512); MODS = SBT("MODS", 144); DER = SBT("DER", 2 * 5 * 8); LG = SBT("LG", 8)
        KDF = SBT("KDF", 16); KDB = SBT("KDB", 16); GEF = SBT("GEF", 32); GEB = SBT("GEB", 32); C512 = SBT("C512", 8)
        COF = SBT("COF", 16); COB = SBT("COB", 16); CTXC = SBT("CTXC", 8); CC = SBT("CC", 32)
        NAR = 33280 + 14336
        ARt = SBT("arena", NAR); AR = Arena(ARt, NAR)
        PS = [es.enter_context(nc.psum_tensor("ps%d" % i, [128, 512], F32))[:, :] for i in range(8)]
        pbc = [0]
        def bank():
            i = pbc[0] % 8; pbc[0] += 1
            return i
        uid = [0]
        def U(s):
            uid[0] += 1; return "%s_%d" % (s, uid[0])
        KB = lambda a: AR.key(a)
        def dma(out, in_, reads=(), writes=(), eng='sp'):
            return P.op(eng, lambda e, o=out, i=in_: e.dma_start(out=o, in_=i), reads=reads, writes=writes, dma=True)
        rot = [0]
        def ew():
            rot[0] += 1
            return ('dve', 'pool')[rot[0] % 2]
        def copy_any(eng, out, in_, reads, writes):
            if eng == 'act':
                P.op('act', lambda e: e.activation(out, in_, AF.Copy), reads=reads, writes=writes)
            else:
                P.op(eng, lambda e: e.tensor_copy(out, in_), reads=reads, writes=writes)
        def MODv(j, kc, w): return MODS[:, (j * 8 + kc) * 2 + w:(j * 8 + kc) * 2 + w + 1]
        def DERv(w, i, kc): return DER[:, (w * 5 + i) * 8 + kc:(w * 5 + i) * 8 + kc + 1]
        def rstd_from(psv, W, outv, tmpv, kin, kout, ktmp):
            P.op('act', lambda e: e.activation(tmpv, psv, AF.Ln, bias=EPS, scale=1.0), reads=kin, writes=[ktmp])
            P.op('act', lambda e: e.activation(outv, tmpv, AF.Exp, scale=-0.5), reads=[ktmp], writes=[kout])

        sbuf_named = {}
        dram_named = dict(XF=XF, XT=XT, HT=HT, H1=H1, UT=UT, QTR=QTR, QTF=QTF, QTB=QTB, KTR=KTR, VV=VV, RG=RG, YT=YT, SGA=SGA, SGB=SGB,
                          UU=UU, SF=SF, SB=SB, ZT=ZT, M1=M1, CT=CT, MT=MT, H2=H2, RCX=RCX, COUTH=COUTH, **{'COUT%d' % i: COUTS[i] for i in range(8)})
        def stage_end(n):
            if stage != n: return
            P.barrier()
            for nm in dumps:
                if nm in dram_named:
                    src = dram_named[nm]
                    dst = nc.dram_tensor("dbg_" + nm, list(src.shape), src.dtype, kind="ExternalOutput").ap()
                    dma(dst, src)
                else:
                    src = sbuf_named[nm]
                    dst = nc.dram_tensor("dbg_" + nm, list(src.shape), src.dtype, kind="ExternalOutput").ap()
                    dma(dst, src)
            P.emit(nc)
            raise _Stop(nc)
        dma(ident, ident_d, writes=['ident'])
        P.op('dve', lambda e: e.tensor_copy(identb, ident), reads=['ident'], writes=['identb'])
        P.op('pool', lambda e: e.memset(ones, 1.0 / 1024.0), writes=['ones'])
        smt = AR.f32(512)
        for b in range(4):
            dma(smt[:, b * 128:(b + 1) * 128], smalls[b * 128:(b + 1) * 128, :], writes=['smt%d' % b])
        for b in range(4):
            P.op('pe', lambda e, b=b: e.transpose(PS[0][:, b * 128:(b + 1) * 128], smt[:, b * 128:(b + 1) * 128], ident),
                 reads=['smt%d' % b, 'ident'], writes=['ps0'])
        P.op('dve', lambda e: e.tensor_copy(SM, PS[0]), reads=['ps0'], writes=['SM'])
        ccin = AR.f32(16)
        ccv = ccin.rearrange("p (k w) -> p k w", w=2)
        P.op('act', lambda e: e.activation(ccv[:, :, 0], SM[:, 72:80], AF.Silu), reads=['SM'], writes=['cc'])
        P.op('act', lambda e: e.activation(ccv[:, :, 1], SM[:, 80:88], AF.Silu), reads=['SM'], writes=['cc'])
        wst = [AR.f32(8192), AR.f32(8192)]
        DG0 = AR.bf16(248 * 128)
        for k in range(31):
            for kc in range(8):
                j = k * 8 + kc; cw = SM[:, 144 + k * 8 + kc:145 + k * 8 + kc]
                if j % 2 == 0:
                    P.op('dve', lambda e: e.tensor_scalar(DG0[:, j * 128:(j + 1) * 128], identb, cw, None, ALU.mult), reads=['identb', 'SM'], writes=['DG0'])
                else:
                    P.op('act', lambda e: e.activation(DG0[:, j * 128:(j + 1) * 128], identb, AF.Copy, scale=cw), reads=['identb', 'SM'], writes=['DG0'])
        pm = bank()
        for j in range(9):
            ws = wst[j % 2]; wsv = ws.rearrange("p (k c) -> p k c", k=8)
            for kc in range(8):
                dma(wsv[:, kc, :], w_mod[kc * 128:(kc + 1) * 128, j * 1024:(j + 1) * 1024], writes=['wst%d' % (j % 2)])
            def f(e, j=j, wsv=wsv):
                r = None
                for oc in range(8):
                    for kc in range(8):
                        c = (j * 8 + oc) * 2
                        r = e.matmul(PS[pm][:, c:c + 2], wsv[:, kc, oc * 128:(oc + 1) * 128], ccv[:, kc, :],
                                     start=(kc == 0), stop=(kc == 7))
                return r
            P.op('pe', f, reads=['wst%d' % (j % 2), 'cc'], writes=['ps%d' % pm])
        for q4 in range(4):
            dma(DG_D[:, q4 * 7936:(q4 + 1) * 7936], DG0[:, q4 * 7936:(q4 + 1) * 7936], reads=['DG0'])
        psmv = PS[pm][:, 0:144].rearrange("p (c w) -> p c w", w=2)
        modsv = MODS.rearrange("p (c w) -> p c w", w=2)
        for w in range(2):
            P.op('dve', lambda e, w=w: e.tensor_tensor(modsv[:, :, w], psmv[:, :, w], SM[:, 0:72], ALU.add),
                 reads=['ps%d' % pm, 'SM'], writes=['MODS'])
        tmp8 = AR.f32(8)
        for w in range(2):
            for i, (jsc, ncol) in enumerate([(1, 88), (None, None), (4, 96), (7, 104), (None, None)]):
                dv = DER[:, (w * 5 + i) * 8:(w * 5 + i) * 8 + 8]
                if jsc is None:
                    jg = 2 if i == 1 else 8
                    src = modsv[:, jg * 8:(jg + 1) * 8, w]
                    P.op('dve', lambda e, dv=dv, src=src: e.tensor_scalar(dv, src, 0.5, None, ALU.mult), reads=['MODS'], writes=['DER'])
                else:
                    src = modsv[:, jsc * 8:(jsc + 1) * 8, w]
                    k1 = KB(tmp8)
                    P.op('dve', lambda e, src=src: e.tensor_scalar(tmp8, src, 1.0, None, ALU.add), reads=['MODS'], writes=[k1])
                    P.op('dve', lambda e, dv=dv, ncol=ncol: e.tensor_tensor(dv, tmp8, SM[:, ncol:ncol + 8], ALU.mult),
                         reads=[k1, 'SM'], writes=['DER'])
        P.barrier(); AR.reset()
        dcy = AR.f32(8); dce = AR.f32(8)
        dma(dcy, decay_d.partition_broadcast(128), writes=['dcy'])
        dma(CC, cconst.partition_broadcast(128), writes=['CC'])
        P.op('act', lambda e: e.activation(dce, dcy, AF.Exp, scale=-1.0), reads=['dcy'], writes=['dce'])
        P.op('act', lambda e: e.activation(dcy, dce, AF.Ln, bias=1.0, scale=1.0), reads=['dce'], writes=['dcl'])
        P.op('dve', lambda e: e.tensor_scalar(LG, dcy, -1.0, None, ALU.mult), reads=['dcl'], writes=['LG'])
        pc = AR.f32(NPC); QDF = AR.f32(2048); QDB = AR.f32(2048); DMT = AR.f32(8192)
        dma(pc, pconst, writes=['pc'])
        def expo(out, in_, col, wk):
            P.op('act', lambda e: e.activation(out, in_, AF.Exp, scale=LG[:, col:col + 1]), reads=['LG', 'pc', 'CC'], writes=[wk])
        for h in range(4):
            expo(KDF.rearrange("p (s h) -> p s h", h=4)[:, :, h], pc[:, 0:4], h, 'KD')
            expo(KDB.rearrange("p (s h) -> p s h", h=4)[:, :, h], pc[:, 4:8], 4 + h, 'KD')
            expo(GEF.rearrange("p (s h) -> p s h", h=4)[:, :, h], pc[:, 8:16], h, 'GE')
            expo(GEB.rearrange("p (s h) -> p s h", h=4)[:, :, h], pc[:, 16:24], 4 + h, 'GE')
            expo(QDF[:, h * 512:(h + 1) * 512], pc[:, 24:536], h, 'QD')
            expo(QDB[:, h * 512:(h + 1) * 512], pc[:, 536:1048], 4 + h, 'QD')
            expo(COF.rearrange("p (r h) -> p r h", h=4)[:, :, h], CC[:, 0:4], h, 'COF0')
            expo(COB.rearrange("p (r h) -> p r h", h=4)[:, :, h], CC[:, 8:12], 4 + h, 'COB0')
            expo(CTXC[:, h:h + 1], CC[:, 16:17], h, 'CTXC')
            expo(CTXC[:, 4 + h:5 + h], CC[:, 17:18], 4 + h, 'CTXC')
        P.op('act', lambda e: e.activation(C512, LG, AF.Exp, scale=512.0), reads=['LG'], writes=['C512'])
        for h in range(4):
            P.op('dve', lambda e, h=h: e.tensor_tensor(COF.rearrange("p (r h) -> p r h", h=4)[:, :, h],
                                                      COF.rearrange("p (r h) -> p r h", h=4)[:, :, h], CC[:, 4:8], ALU.mult),
                 reads=['COF0', 'CC'], writes=['COF'])
            P.op('dve', lambda e, h=h: e.tensor_tensor(COB.rearrange("p (r h) -> p r h", h=4)[:, :, h],
                                                      COB.rearrange("p (r h) -> p r h", h=4)[:, :, h], CC[:, 12:16], ALU.mult),
                 reads=['COB0', 'CC'], writes=['COB'])
        delta = pc[:, 1048:3096]
        dp = AR.f32(2048); dn = AR.f32(2048); mge = AR.f32(2048); mlt = AR.f32(2048); fb = AR.f32(2048); bb = AR.f32(2048)
        P.op('dve', lambda e: e.tensor_scalar(dp, delta, 0.0, None, ALU.max), reads=['pc'], writes=['dp'])
        P.op('dve', lambda e: e.tensor_scalar(dn, delta, -1.0, 0.0, ALU.mult, ALU.max), reads=['pc'], writes=['dn'])
        P.op('pool', lambda e: e.tensor_single_scalar(mge, delta, 0.0, ALU.is_ge), reads=['pc'], writes=['mge'])
        P.op('pool', lambda e: e.tensor_single_scalar(mlt, delta, 0.0, ALU.is_lt), reads=['pc'], writes=['mlt'])
        for h in range(4):
            kf = KB(fb); kb = KB(bb)
            P.op('act', lambda e, h=h: e.activation(fb, dp, AF.Exp, scale=LG[:, h:h + 1]), reads=['dp', 'LG'], writes=[kf])
            P.op('act', lambda e, h=h: e.activation(bb, dn, AF.Exp, scale=LG[:, 4 + h:5 + h]), reads=['dn', 'LG'], writes=[kb])
            P.op('dve', lambda e: e.tensor_tensor(fb, fb, mge, ALU.mult), reads=[kf, 'mge'], writes=[kf])
            P.op('pool', lambda e: e.tensor_tensor(bb, bb, mlt, ALU.mult), reads=[kb, 'mlt'], writes=[kb])
            P.op('dve', lambda e, h=h: e.tensor_tensor(DMT[:, h * 2048:(h + 1) * 2048], fb, bb, ALU.add),
                 reads=[kf, kb], writes=['DMT'])
        dma(QD_D[0], QDF, reads=['QD']); dma(QD_D[1], QDB, reads=['QD']); dma(DMT_D, DMT, reads=['DMT'])
        P.barrier(); AR.reset()

        def norm_A(xf, W, kx, sqb):
            xfv = xf.rearrange("p (k t) -> p k t", k=8); sqv = sqb.rearrange("p (k t) -> p k t", k=8)
            ksq = KB(sqb)
            P.op('act', lambda e: e.activation(sqv[:, :, 0:W], xfv[:, :, 0:W], AF.Square), reads=[kx], writes=[ksq])
            b = bank()
            def f(e):
                r = None
                for kc in range(8):
                    r = e.matmul(PS[b][:, 0:W], ones, sqv[:, kc, 0:W], start=(kc == 0), stop=(kc == 7))
                return r
            P.op('pe', f, reads=[ksq, 'ones'], writes=['ps%d' % b])
            return b
        def norm_B(xf, W, Af, Bf, dstXT, t0, kx, ms, kms, rs, tmps, xm):
            xfv = xf.rearrange("p (k t) -> p k t", k=8); xmv = xm.rearrange("p (k t) -> p k t", k=8)
            krs = KB(rs)
            rstd_from(ms, W, rs[:, 0:W], tmps[0][:, 0:W], kms, krs, KB(tmps[0]))
            kxm = KB(xm)
            for kc in range(8):
                tp = tmps[1 + kc % 2]; kt = KB(tp)
                P.op('dve', lambda e, kc=kc, tp=tp: e.tensor_tensor(tp[:, 0:W], xfv[:, kc, 0:W], rs[:, 0:W], ALU.mult),
                     reads=[kx, krs], writes=[kt])
                P.op('act', lambda e, kc=kc, tp=tp: e.activation(xmv[:, kc, 0:W], tp[:, 0:W], AF.Identity, bias=Bf(kc), scale=Af(kc)),
                     reads=[kt, 'MODS', 'DER'], writes=[kxm])
            dma(dstXT[:, :, t0:t0 + W].rearrange("k p t -> p k t"), xmv[:, :, 0:W], reads=[kxm])
        def norm_tile(xf, W, Af, Bf, dstXT, t0, kx, sqb, rs, tmps, xm):
            b = norm_A(xf, W, kx, sqb)
            norm_B(xf, W, Af, Bf, dstXT, t0, kx, PS[b][:, 0:W], ['ps%d' % b], rs, tmps, xm)

        def norm_pass(src_fm, dstXT, tiles, ai, bj, first=False, next_w=None):
            P.barrier(); AR.reset()
            set_limit({0} if next_w is not None else set())
            bg = None; per = 0
            if next_w is not None:
                KCn, nun, accsn = next_w
                bg = wgen(0, KCn, nun, accsn, make_stg()); per = -(-nchunks(KCn, nun, accsn) // len(tiles))
            assert first
            xfs = [AR.f32(4096), AR.f32(4096)]; sqbs = [AR.f32(4096), AR.f32(4096)]; rs = AR.f32(512); mss = [AR.f32(512), AR.f32(512)]
            tmps = [AR.f32(512) for _ in range(3)]; xms = [AR.bf16(4096), AR.bf16(4096)]
            xtoks = [AR.f32(4096), AR.f32(4096)]
            def stageA(ti):
                t0, W = tiles[ti]; ns = W // 128
                xf = xfs[ti % 2]; xfv = xf.rearrange("p (k t) -> p k t", k=8); kx = KB(xf)
                xtok = xtoks[ti % 2]; xtv = xtok.rearrange("p (s d) -> p s d", d=1024); ktk = KB(xtok)
                dma(xtv[:, 0:ns, :], xin[t0:t0 + W, :].rearrange("(s p) d -> p s d", p=128), writes=[ktk])
                for kc in range(8):
                    b = bank()
                    def f(e, kc=kc, b=b):
                        r = None
                        for s_ in range(ns):
                            r = e.transpose(PS[b][:, s_ * 128:(s_ + 1) * 128], xtv[:, s_, kc * 128:(kc + 1) * 128], ident)
                        return r
                    P.op('pe', f, reads=[ktk, 'ident'], writes=['ps%d' % b])
                    copy_any(('act', 'dve')[kc % 2], xfv[:, kc, 0:W], PS[b][:, 0:W], ['ps%d' % b], [kx])
                dma(XF[:, :, t0:t0 + W].rearrange("k p t -> p k t"), xfv[:, :, 0:W], reads=[kx])
                b = norm_A(xf, W, kx, sqbs[ti % 2])
                ms = mss[ti % 2]
                P.op('dve', lambda e: e.tensor_copy(ms[:, 0:W], PS[b][:, 0:W]), reads=['ps%d' % b], writes=[KB(ms)])
            def stageB(ti):
                t0, W = tiles[ti]; w = 1 if t0 >= T else 0
                xf = xfs[ti % 2]; ms = mss[ti % 2]
                norm_B(xf, W, lambda kc: DERv(w, ai, kc), lambda kc: MODv(bj, kc, w), dstXT, t0, KB(xf), ms[:, 0:W], [KB(ms)], rs, tmps, xms[ti % 2])
            stageA(0)
            for ti in range(len(tiles)):
                if ti + 1 < len(tiles): stageA(ti + 1)
                stageB(ti)
                if bg is not None: drain(bg, per)
            drain(bg)

        SLOTW = 11264
        SLOTS = [ARt[:, NAR - (i + 1) * SLOTW:NAR - i * SLOTW].bitcast(BF16) for i in range(2)]
        SKEY = ['SLOT0', 'SLOT1']
        def set_limit(live):
            AR.limit = NAR - 2 * SLOTW if 1 in live else (NAR - SLOTW if 0 in live else NAR)
        def wviews(slot, KC, nu, naccs):
            return [SLOTS[slot][:, a * KC * nu * 128:(a + 1) * KC * nu * 128].rearrange("p (k c) -> p k c", k=KC) for a in range(naccs)]
        def wgen(slot, KC, nu, accs, stg):
            wkey = SKEY[slot]; views = wviews(slot, KC, nu, len(accs))
            chunks = []
            for ai_, (Wd, col0, scale, mode) in enumerate(accs):
                for kc in range(KC):
                    for c0 in range(0, nu * 128, 1024):
                        chunks.append((views[ai_], Wd, kc, col0, c0, min(1024, nu * 128 - c0), scale, mode))
            L = len(stg); D = L - 1
            def dma_chunk(i):
                v, Wd, kc, col0, c0, n, scale, mode = chunks[i]; sb = stg[i % L]
                dma(sb[:, 0:n], Wd[kc * 128:(kc + 1) * 128, col0 + c0:col0 + c0 + n], writes=[KB(sb)])
            def cast_chunk(i):
                v, Wd, kc, col0, c0, n, scale, mode = chunks[i]; sb = stg[i % L]; ks = KB(sb)
                o = v[:, kc, c0:c0 + n]
                if mode == 'plain':
                    if i % 2 == 0:
                        P.op('act', lambda e: e.activation(o, sb[:, 0:n], AF.Copy, scale=float(scale)), reads=[ks], writes=[wkey])
                    else:
                        P.op('dve', lambda e: e.tensor_scalar(o, sb[:, 0:n], float(scale), None, ALU.mult), reads=[ks], writes=[wkey])
                else:
                    ov = o.rearrange("p (b h c) -> p b h c", h=2, c=64); sv = sb[:, 0:n].rearrange("p (b h c) -> p b h c", h=2, c=64)
                    P.op('dve', lambda e: e.tensor_scalar(ov[:, :, 0, :], sv[:, :, 1, :], -float(scale), None, ALU.mult), reads=[ks], writes=[wkey])
                    P.op('act', lambda e: e.activation(ov[:, :, 1, :], sv[:, :, 0, :], AF.Copy, scale=float(scale)), reads=[ks], writes=[wkey])
            for i in range(min(D, len(chunks))): dma_chunk(i)
            for i in range(len(chunks)):
                if i + D < len(chunks): dma_chunk(i + D)
                cast_chunk(i)
                yield len(chunks) - i - 1
        def drain(g, n=None):
            if g is None: return
            k = 0
            for _ in g:
                k += 1
                if n is not None and k >= n: return
        def nchunks(KC, nu, accs): return len(accs) * KC * ((nu * 128 + 1023) // 1024)
        def make_stg():
            stgall = AR.f32(4096)
            AR.stgall = stgall
            return [AR.f32_sub(stgall, i * 1024, 1024) for i in range(4)]

        def gemm_fm(SRC, KC, groups, tiles, evac, pre_tile=None, post_tile=None, extra_alloc=None, prefetch=True,
                    slot=0, preloaded=False, next_w=None):
            P.barrier(); AR.reset()
            G = len(groups)
            live = {(slot + g) % 2 for g in range(G)}
            nslot = (slot + G) % 2
            if next_w is not None: live.add(nslot)
            set_limit(live)
            stg = make_stg()
            xts = [AR.bf16(KC * 512), AR.bf16(KC * 512)]
            ctx = extra_alloc() if extra_alloc else None
            if not preloaded:
                drain(wgen(slot, KC, groups[0][0], groups[0][1], stg))
            for gi, (nu, accs) in enumerate(groups):
                gslot = (slot + gi) % 2
                wkey = SKEY[gslot]
                WBs = wviews(gslot, KC, nu, len(accs))
                bg = None; per = 0
                if gi + 1 < G:
                    bg = wgen((slot + gi + 1) % 2, KC, groups[gi + 1][0], groups[gi + 1][1], stg)
                    per = -(-nchunks(KC, groups[gi + 1][0], groups[gi + 1][1]) // len(tiles))
                elif next_w is not None:
                    KCn, nun, accsn = next_w
                    bg = wgen(nslot, KCn, nun, accsn, stg); per = -(-nchunks(KCn, nun, accsn) // len(tiles))
                def issue(ti):
                    t0, W = tiles[ti]
                    xt = xts[ti % 2].rearrange("p (k t) -> p k t", k=KC); kx = KB(xts[ti % 2])
                    dma(xt[:, :, 0:W], SRC[:, :, t0:t0 + W].rearrange("k p t -> p k t"), writes=[kx])
                    if pre_tile: pre_tile(ti, t0, W, gi, ctx)
                ahead = prefetch
                if ahead: issue(0)
                for ti, (t0, W) in enumerate(tiles):
                    xt = xts[ti % 2].rearrange("p (k t) -> p k t", k=KC); kx = KB(xts[ti % 2])
                    if ahead:
                        if ti + 1 < len(tiles): issue(ti + 1)
                    else:
                        issue(ti)
                    for u in range(nu):
                        pss = []
                        for ai_ in range(len(accs)):
                            b = bank()
                            def f(e, b=b, v=WBs[ai_], u=u, xt=xt, W=W):
                                r = None
                                for kc in range(KC):
                                    r = e.matmul(PS[b][:, 0:W], v[:, kc, u * 128:(u + 1) * 128], xt[:, kc, 0:W],
                                                 start=(kc == 0), stop=(kc == KC - 1))
                                return r
                            P.op('pe', f, reads=[wkey, kx], writes=['ps%d' % b])
                            pss.append(b)
                        evac(ti, t0, W, gi, u, pss, ctx)
                        if bg is not None and per and u == nu // 2: drain(bg, (per + 1) // 2)
                    if bg is not None and per: drain(bg, per - (per + 1) // 2 if per > 1 else 0)
                    if post_tile: post_tile(ti, t0, W, gi, ctx)
                drain(bg)

        def gemm_tm(SRC, Wd, col0, tiles, evac, slot=0, preloaded=False, next_w=None):
            P.barrier(); AR.reset()
            live = {slot}
            if next_w is not None: live.add(1 - slot)
            set_limit(live)
            stg = make_stg()
            WB = wviews(slot, 8, 16, 1)[0]
            xts = [AR.bf16(4096), AR.bf16(4096)]
            sts = [AR.bf16(512) for _ in range(4)]
            AR.tm_tmp = [AR.f32(512) for _ in range(4)]; AR.tm_gnw = AR.f32(2048)
            dma(AR.tm_gnw, gnw_d.partition_broadcast(128), writes=[KB(AR.tm_gnw)])
            wkey = SKEY[slot]
            if not preloaded:
                drain(wgen(slot, 8, 16, [(Wd, col0, 1.0, 'plain')], stg))
            bg = None; per = 0
            if next_w is not None:
                KCn, nun, accsn = next_w
                bg = wgen(1 - slot, KCn, nun, accsn, stg); per = -(-nchunks(KCn, nun, accsn) // len(tiles))
            n = [0]
            def ld_x(ti):
                t0, W = tiles[ti]
                dma(xts[ti % 2].rearrange("p (k t) -> p k t", k=8)[:, :, 0:W], SRC[:, :, t0:t0 + W].rearrange("k p t -> p k t"), writes=[KB(xts[ti % 2])])
            ld_x(0)
            for ti, (t0, W) in enumerate(tiles):
                xt = xts[ti % 2].rearrange("p (k t) -> p k t", k=8); kx = KB(xts[ti % 2])
                if ti + 1 < len(tiles): ld_x(ti + 1)
                for s in range(W // 128):
                    for cb in range(4):
                        b = bank()
                        def f(e, b=b, s=s, cb=cb, xt=xt):
                            r = None
                            for kc in range(8):
                                r = e.matmul(PS[b], xt[:, kc, s * 128:(s + 1) * 128], WB[:, kc, cb * 512:(cb + 1) * 512],
                                             start=(kc == 0), stop=(kc == 7))
                            return r
                        P.op('pe', f, reads=[wkey, kx], writes=['ps%d' % b])
                        st = sts[n[0] % 4]; n[0] += 1
                        evac(t0, s, cb, b, st)
                    if bg is not None and per: drain(bg, -(-per // (W // 128)))
            drain(bg)
        def ffn(XTsrc, w_i, w_o_, tiles, RES, gi_, OUT, final=False, norm_next=None, pre_in=True):
            def alloc1():
                return {'tmp': [AR.f32(512) for _ in range(3)], 'st': [AR.bf16(512) for _ in range(3)], 'n': [0]}
            def ev1(ti, t0, W, gi, u, pss, c):
                i = c['n'][0] % 3; c['n'][0] += 1
                tmp = c['tmp'][i]; st = c['st'][i]; kt = KB(tmp); ks = KB(st)
                P.op('act', lambda e: e.activation(tmp[:, 0:W], PS[pss[0]][:, 0:W], AF.Silu), reads=['ps%d' % pss[0]], writes=[kt])
                P.op('dve', lambda e: e.tensor_tensor(st[:, 0:W], tmp[:, 0:W], PS[pss[1]][:, 0:W], ALU.mult),
                     reads=[kt, 'ps%d' % pss[1]], writes=[ks])
                dma(HT[gi * 11 + u, :, t0:t0 + W], st[:, 0:W], reads=[ks])
            gemm_fm(XTsrc, 8, [(11, [(w_i, g * 1408, 1.0, 'plain'), (w_i, DFF + g * 1408, 1.0, 'plain')]) for g in range(2)],
                    tiles, ev1, extra_alloc=alloc1, slot=0, preloaded=pre_in, next_w=(KF, 8, [(w_o_, 0, 1.0, 'plain')]))
            def alloc2():
                c = {'res': [AR.f32(4096), AR.f32(4096)], 'st': [AR.f32(512) for _ in range(3)], 'n': [0], 'k': {}}
                if final:
                    c['h3'] = AR.f32(4096); c['sq'] = AR.stgall; c['rs'] = AR.f32(512); c['tp'] = c['st']
                else:
                    c['sq'] = AR.stgall; c['rs'] = AR.f32(512); c['xm'] = AR.bf16(4096)
                return c
            def pre2(ti, t0, W, gi, c):
                r = c['res'][ti % 2].rearrange("p (k t) -> p k t", k=8); k = KB(c['res'][ti % 2]); c['k'][ti] = k
                dma(r[:, :, 0:W], RES[:, :, t0:t0 + W].rearrange("k p t -> p k t"), writes=[k])
            def ev2(ti, t0, W, gi, u, pss, c):
                w = 1 if t0 >= T else 0
                r = c['res'][ti % 2].rearrange("p (k t) -> p k t", k=8)
                if final:
                    h3 = c['h3'].rearrange("p (k t) -> p k t", k=8)
                    P.op('dve', lambda e: e.scalar_tensor_tensor(h3[:, u, 0:W], PS[pss[0]][:, 0:W], DERv(w, gi_, u), r[:, u, 0:W], ALU.mult, ALU.add),
                         reads=['ps%d' % pss[0], c['k'][ti], 'DER'], writes=[KB(c['h3'])])
                else:
                    P.op('dve', lambda e: e.scalar_tensor_tensor(r[:, u, 0:W], PS[pss[0]][:, 0:W], DERv(w, gi_, u), r[:, u, 0:W], ALU.mult, ALU.add),
                         reads=['ps%d' % pss[0], c['k'][ti], 'DER'], writes=[c['k'][ti]])
            def post2(ti, t0, W, gi, c):
                if not final:
                    rb = c['res'][ti % 2]; r = rb.rearrange("p (k t) -> p k t", k=8); w = 1 if t0 >= T else 0
                    dma(OUT[:, :, t0:t0 + W].rearrange("k p t -> p k t"), r[:, :, 0:W], reads=[c['k'][ti]])
                    ai, bj, dst = norm_next
                    norm_tile(rb, W, lambda kc: DERv(w, ai, kc), lambda kc: MODv(bj, kc, w), dst, t0, c['k'][ti], c['sq'], c['rs'], c['st'], c['xm'])
                    return
                h3 = c['h3'].rearrange("p (k t) -> p k t", k=8); sqv = c['sq'].rearrange("p (k t) -> p k t", k=8)
                kh = KB(c['h3']); ksq = KB(c['sq'])
                P.op('act', lambda e: e.activation(sqv, h3, AF.Square), reads=[kh], writes=[ksq])
                b = bank()
                def f(e):
                    r = None
                    for kc in range(8):
                        r = e.matmul(PS[b], ones, sqv[:, kc, :], start=(kc == 0), stop=(kc == 7))
                    return r
                P.op('pe', f, reads=[ksq, 'ones'], writes=['ps%d' % b])
                krs = KB(c['rs'])
                rstd_from(PS[b], W, c['rs'], c['tp'][0], ['ps%d' % b], krs, KB(c['tp'][0]))
                ky = KB(c['sq'])
                for kc in range(8):
                    tp = c['tp'][1 + kc % 2]; kt = KB(tp)
                    P.op('dve', lambda e, kc=kc, tp=tp: e.tensor_tensor(tp, h3[:, kc, :], c['rs'], ALU.mult), reads=[kh, krs], writes=[kt])
                    P.op('act', lambda e, kc=kc, tp=tp: e.activation(sqv[:, kc, :], tp, AF.Copy, scale=SM[:, 112 + kc:113 + kc]),
                         reads=[kt, 'SM'], writes=[ky])
                otv = c['res'][ti % 2].rearrange("p (s d) -> p s d", d=1024); ko = KB(c['res'][ti % 2])
                for s in range(4):
                    for half in range(2):
                        b2 = bank()
                        def f2(e, s=s, half=half, b2=b2):
                            r = None
                            for q in range(4):
                                kc = half * 4 + q
                                r = e.transpose(PS[b2][:, q * 128:(q + 1) * 128], sqv[:, kc, s * 128:(s + 1) * 128], ident)
                            return r
                        P.op('pe', f2, reads=[ky, 'ident'], writes=['ps%d' % b2])
                        copy_any(('act', 'dve')[half], otv[:, s, half * 512:(half + 1) * 512], PS[b2], ['ps%d' % b2], [ko])
                dma(out_d[t0:t0 + 512, :].rearrange("(s p) d -> p s d", p=128), otv, reads=[ko])
            gemm_fm(HT, KF, [(8, [(w_o_, 0, 1.0, 'plain')])], tiles, ev2, pre_tile=pre2, post_tile=post2, extra_alloc=alloc2, slot=0, preloaded=True)

        sbuf_named.update(SM=SM, MODS=MODS, DER=DER, LG=LG, KDF=KDF, KDB=KDB, GEF=GEF, GEB=GEB, C512=C512, COF=COF, COB=COB, CTXC=CTXC,
                          CC=CC)
        stage_end(0)
        norm_pass(None, XT, TILES, 0, 0, first=True, next_w=(8, 11, [(w1i, 0, 1.0, 'plain'), (w1i, DFF, 1.0, 'plain')]))
        stage_end(1)
        ffn(XT, w1i, w1o, TILES, XF, 1, H1, norm_next=(2, 3, UT))
        stage_end(3)

        def allocq():
            c = {'tab': [AR.f32(2048), AR.f32(2048)], 'tmp': [AR.f32(512) for _ in range(6)],
                 'st': [AR.bf16(512) for _ in range(6)], 'n': [0], 'k': {}, 'qdf': AR.f32(2048), 'qdb': AR.f32(2048)}
            dma(c['qdf'], QD_D[0], writes=[KB(c['qdf'])]); dma(c['qdb'], QD_D[1], writes=[KB(c['qdb'])])
            return c
        def preq(ti, t0, W, gi, c):
            tb = c['tab'][ti % 2].rearrange("p (f t) -> p f t", f=4); k = KB(c['tab'][ti % 2]); c['k'][ti] = k
            dma(tb[:, :, 0:W], ropetab[:, :, t0:t0 + W].rearrange("f p t -> p f t"), writes=[k])
        def mk_evqk(isq):
            def ev(ti, t0, W, gi, u, pss, c):
                tb = c['tab'][ti % 2].rearrange("p (f t) -> p f t", f=4)
                i = c['n'][0] % 2; c['n'][0] += 1
                t1, t2, t3 = c['tmp'][i * 3:(i + 1) * 3]; f0 = (u % 2) * 2; h = u // 2
                k1 = KB(t1); k2 = KB(t2); k3 = KB(t3)
                P.op('dve', lambda e: e.tensor_tensor(t1[:, 0:W], PS[pss[0]][:, 0:W], tb[:, f0, 0:W], ALU.mult), reads=['ps%d' % pss[0], c['k'][ti]], writes=[k1])
                P.op('dve', lambda e: e.tensor_tensor(t2[:, 0:W], PS[pss[1]][:, 0:W], tb[:, f0 + 1, 0:W], ALU.mult), reads=['ps%d' % pss[1], c['k'][ti]], writes=[k2])
                if isq:
                    P.op('dve', lambda e: e.tensor_tensor(t3[:, 0:W], t1[:, 0:W], t2[:, 0:W], ALU.add), reads=[k1, k2], writes=[k3])
                    s0, s1, s2 = c['st'][i * 3:(i + 1) * 3]
                    ka = KB(s0); kb = KB(s1); kc_ = KB(s2)
                    P.op('act', lambda e: e.activation(s0[:, 0:W], t3[:, 0:W], AF.Copy), reads=[k3], writes=[ka])
                    P.op('dve', lambda e: e.tensor_tensor(s1[:, 0:W], t3[:, 0:W], c['qdf'][:, h * 512:h * 512 + W], ALU.mult), reads=[k3, KB(c['qdf'])], writes=[kb])
                    P.op('pool', lambda e: e.tensor_tensor(s2[:, 0:W], t3[:, 0:W], c['qdb'][:, h * 512:h * 512 + W], ALU.mult), reads=[k3, KB(c['qdb'])], writes=[kc_])
                    dma(QTR[u, :, t0:t0 + W], s0[:, 0:W], reads=[ka]); dma(QTF[u, :, t0:t0 + W], s1[:, 0:W], reads=[kb])
                    dma(QTB[u, :, t0:t0 + W], s2[:, 0:W], reads=[kc_])
                else:
                    s0 = c['st'][i * 3]; ka = KB(s0)
                    P.op('pool', lambda e: e.tensor_tensor(s0[:, 0:W], t1[:, 0:W], t2[:, 0:W], ALU.add), reads=[k1, k2], writes=[ka])
                    dma(KTR[u, :, t0:t0 + W], s0[:, 0:W], reads=[ka])
            return ev
        run_q = lambda: gemm_fm(UT, 8, [(8, [(w_in, 0, 1.0, 'plain'), (w_in, 0, 1.0, 'rot')])], MAIN, mk_evqk(True), pre_tile=preq, extra_alloc=allocq, slot=1, preloaded=True)
        gemm_fm(UT, 8, [(8, [(w_in, K0, 0.0625, 'plain'), (w_in, K0, 0.0625, 'rot')])], TILES, mk_evqk(False), pre_tile=preq, extra_alloc=allocq,
                slot=0, preloaded=False, next_w=(8, 16, [(w_in, V0, 1.0, 'plain')]))
        def evv(t0, s, cb, b, st):
            k = KB(st)
            P.op(('act', 'dve')[cb % 2] if False else 'act', lambda e: e.activation(st, PS[b], AF.Copy), reads=['ps%d' % b], writes=[k])
            dma(VV[t0 + s * 128:t0 + (s + 1) * 128, cb * 512:(cb + 1) * 512], st, reads=[k])
        gemm_tm(UT, w_in, V0, TILES, evv, slot=1, preloaded=True, next_w=(8, 16, [(w_in, G0, 1.0, 'plain')]))
        rgn = [0]
        def evrg(t0, s, cb, b, st):
            k = KB(st); tmp = AR.tm_tmp[rgn[0] % 4]; rgn[0] += 1; kt = KB(tmp)
            P.op('act', lambda e: e.activation(tmp, PS[b], AF.Silu), reads=['ps%d' % b], writes=[kt])
            P.op('dve', lambda e: e.tensor_tensor(st, tmp, AR.tm_gnw[:, cb * 512:(cb + 1) * 512], ALU.mult), reads=[kt, KB(AR.tm_gnw)], writes=[k])
            dma(RG[t0 + s * 128:t0 + (s + 1) * 128, cb * 512:(cb + 1) * 512], st, reads=[k])
        run_rg = lambda: gemm_tm(UT, w_in, G0, MAIN, evrg, slot=0, preloaded=True, next_w=(8, 8, [(w_in, C0, 1.0, 'plain'), (w_in, C0 + 1024, 1.0, 'plain')]))
        def alloccv():
            return {'tmp': [AR.f32(512) for _ in range(3)], 'st': [AR.f32(512) for _ in range(3)], 'sb': [AR.bf16(512) for _ in range(3)], 'n': [0]}
        def evcv(ti, t0, W, gi, u, pss, c):
            i = c['n'][0] % 3; c['n'][0] += 1
            tmp = c['tmp'][i]; st = c['sb'][i]; kt = KB(tmp); ks = KB(st)
            P.op('act', lambda e: e.activation(tmp, PS[pss[1]], AF.Sigmoid), reads=['ps%d' % pss[1]], writes=[kt])
            P.op('dve', lambda e: e.tensor_tensor(st, tmp, PS[pss[0]], ALU.mult), reads=[kt, 'ps%d' % pss[0]], writes=[ks])
            dma(YT[u, :, 16 + t0:16 + t0 + W], st, reads=[ks])
        run_cv = lambda: gemm_fm(UT, 8, [(8, [(w_in, C0, 1.0, 'plain'), (w_in, C0 + 1024, 1.0, 'plain')])], MAIN, evcv, extra_alloc=alloccv,
                                 slot=1, preloaded=True, next_w=(8, 16, [(w_in, GA0, 1.0, 'plain')]))
        def evg(ti, t0, W, gi, u, pss, c):
            i = c['n'][0] % 3; c['n'][0] += 1
            st = c['st'][i]; ks = KB(st)
            P.op('act', lambda e: e.activation(st, PS[pss[0]], AF.Sigmoid), reads=['ps%d' % pss[0]], writes=[ks])
            dma((SGA if u < 8 else SGB)[u % 8, :, t0:t0 + W], st, reads=[ks])
        run_gates = lambda: gemm_fm(UT, 8, [(16, [(w_in, GA0, 1.0, 'plain')])], MAIN, evg, extra_alloc=alloccv,
                                    slot=0, preloaded=True, next_w=(8, 8, [(w_in, 0, 1.0, 'plain'), (w_in, 0, 1.0, 'rot')]))

        stage_end(4)
        P.barrier(); AR.reset(); set_limit({0})
        Ef = AR.f32(4096); Eb = AR.f32(4096)
        ktile = AR.bf16(4096); vtile = AR.bf16(8192); khf = AR.bf16(4096); khb = AR.bf16(4096)
        ust = [AR.f32(512) for _ in range(4)]
        P.op('pool', lambda e: e.memset(Ef, 0.0), writes=['Ef']); P.op('pool', lambda e: e.memset(Eb, 0.0), writes=['Eb'])
        ktv = ktile.rearrange("p (k t) -> p k t", k=8); vtv = vtile.rearrange("p (s d) -> p s d", d=2048)
        khv = [khf.rearrange("p (s d) -> p s d", d=1024), khb.rearrange("p (s d) -> p s d", d=1024)]
        KDv = [KDF.rearrange("p (s h) -> p s h", h=4), KDB.rearrange("p (s h) -> p s h", h=4)]
        GEv = [GEF.rearrange("p (s h) -> p s h", h=4), GEB.rearrange("p (s h) -> p s h", h=4)]
        Ev = [Ef.rearrange("p (g e) -> p g e", e=512), Eb.rearrange("p (g e) -> p g e", e=512)]
        un = 0
        for ti, (t0, W) in enumerate(TILES):
            ns = W // 128; isctx = t0 >= T
            kk = KB(ktile); kv = KB(vtile)
            dma(ktv[:, :, 0:W], KTR[:, :, t0:t0 + W].rearrange("k p t -> p k t"), writes=[kk])
            dma(vtv[:, 0:ns, :], VV[t0:t0 + W, :].rearrange("(s p) d -> p s d", p=128), writes=[kv])
            kkh = [KB(khf), KB(khb)]
            for s in range(ns):
                b = bank(); pb = PS[b].bitcast(BF16)
                def f(e, s=s, pb=pb):
                    r = None
                    for j in range(8):
                        r = e.transpose(pb[:, j * 128:(j + 1) * 128], ktv[:, j, s * 128:(s + 1) * 128], identb)
                    return r
                P.op('pe', f, reads=[kk, 'identb'], writes=['ps%d' % b])
                for d in range(2):
                    slot = (s + 2) if (isctx and d == 0) else s
                    pbv = pb[:, 0:1024].rearrange("p (h c) -> p h c", h=4)
                    o = khv[d][:, s, :].rearrange("p (h c) -> p h c", h=4)
                    sc = KDv[d][:, slot, :].unsqueeze(2).broadcast_to([128, 4, 256])
                    P.op('dve', lambda e, o=o, pbv=pbv, sc=sc: e.tensor_tensor(o, pbv, sc, ALU.mult), reads=['ps%d' % b, 'KD'], writes=[kkh[d]])
            for d in range(2):
                for hd in range(8):
                    h = hd // 2
                    b = bank()
                    def f(e, d=d, hd=hd, h=h, b=b):
                        r = None
                        for s in range(ns):
                            r = e.matmul(PS[b], khv[d][:, s, hd * 128:(hd + 1) * 128], vtv[:, s, h * 512:(h + 1) * 512],
                                         start=(s == 0), stop=(s == ns - 1))
                        return r
                    P.op('pe', f, reads=[kkh[d], kv], writes=['ps%d' % b])
                    st = ust[un % 4]; un += 1; ks = KB(st)
                    P.op('act', lambda e, st=st, b=b: e.activation(st, PS[b], AF.Copy), reads=['ps%d' % b], writes=[ks])
                    if isctx:
                        dma(RCX[d, hd], st, reads=[ks])
                    else:
                        dma(UU[ti, d, hd], st, reads=[ks])
                        ek = 'Ef' if d == 0 else 'Eb'
                        P.op('dve', lambda e, d=d, hd=hd, h=h, b=b, ti=ti: e.scalar_tensor_tensor(Ev[d][:, hd, :], PS[b], GEv[d][:, ti, h:h + 1], Ev[d][:, hd, :], ALU.mult, ALU.add),
                             reads=['ps%d' % b, 'GE', ek], writes=[ek])
        for d in range(2):
            for h in range(4):
                dma(CINS[d * 4 + h].rearrange("(g p) e -> p g e", p=128), Ev[d][:, 2 * h:2 * h + 2, :], reads=['Ef' if d == 0 else 'Eb'], writes=['CIN'])
        stage_end(5)
        def allgather(ci_, co_, kin, kout):
            P.op('pool', lambda e: e.collective_compute("AllGather", ALU.bypass, replica_groups=[[0, 1, 2, 3], [4, 5, 6, 7]],
                                                        ins=[ci_.opt()], outs=[co_.opt()]), reads=[kin], writes=[kout], cc=True)
        for ci_, co_ in zip(CINS, COUTS): allgather(ci_, co_, 'CIN', 'COUT')
        run_rg(); run_cv()
        P.barrier()
        dma(CINH[0:1024, 0:16].rearrange("(k p) c -> k p c", p=128), YT[:, :, 16:32], writes=['CINH'])
        dma(CINH[1024:2048, 0:16].rearrange("(k p) c -> k p c", p=128), YT[:, :, T:16 + T], writes=['CINH'])
        allgather(CINH, COUTH, 'CINH', 'COUTH')
        run_gates(); run_q()
        stage_end(6)
        P.barrier(); AR.reset(); set_limit(set())
        cur = [AR.f32(4096), AR.f32(4096)]; gls = [[AR.f32(4096), AR.f32(4096)], [AR.f32(4096), AR.f32(4096)]]; sbf = [[AR.bf16(4096), AR.bf16(4096)] for _ in range(2)]
        hg = AR.bf16(4 * 2 * 8 * 16); pads = AR.bf16(2 * 8 * 16)
        COv = [COF.rearrange("p (r h) -> p r h", h=4), COB.rearrange("p (r h) -> p r h", h=4)]
        G3 = lambda a: a.rearrange("p (g e) -> p g e", e=512)
        gcnt = [0, 0]
        def nextg(d):
            g = gls[d][gcnt[d] % 2]; gcnt[d] += 1
            return g
        cvs = [G3(cur[0]), G3(cur[1])]; cks = ['cur0', 'cur1']
        for d in range(2):
            g0 = nextg(d); kg = KB(g0)
            dma(G3(g0), RCX[d].rearrange("g p e -> p g e"), writes=[kg])
            for h in range(4):
                P.op('dve', lambda e, g0=g0, h=h, d=d: e.tensor_scalar(cvs[d][:, 2 * h:2 * h + 2, :], G3(g0)[:, 2 * h:2 * h + 2, :],
                                                                      CTXC[:, 4 * d + h:4 * d + h + 1], None, ALU.mult), reads=[kg, 'CTXC'], writes=[cks[d]])
        def ld_rank(r, d):
            g1 = nextg(d)
            for h in range(4):
                dma(G3(g1)[:, 2 * h:2 * h + 2, :], COUTS[d * 4 + h][r * 256:(r + 1) * 256, :].rearrange("(g p) e -> p g e", p=128), reads=['COUT'], writes=[KB(g1)])
            return g1
        def ld_u(i, d):
            g1 = nextg(d)
            dma(G3(g1), UU[i, d].rearrange("g p e -> p g e"), writes=[KB(g1)])
            return g1
        tidx = lambda n_, d: n_ if d == 0 else 7 - n_
        pend = {d: ld_rank(0, d) for d in range(2)}
        for r in range(4):
            for d in range(2):
                g1 = pend[d]; kg = KB(g1)
                pend[d] = ld_rank(r + 1, d) if r < 3 else ld_u(tidx(0, d), d)
                for h in range(4):
                    P.op('dve', lambda e, g1=g1, h=h, d=d, r=r: e.scalar_tensor_tensor(cvs[d][:, 2 * h:2 * h + 2, :], G3(g1)[:, 2 * h:2 * h + 2, :],
                                                                                     COv[d][:, r, h:h + 1], cvs[d][:, 2 * h:2 * h + 2, :], ALU.mult, ALU.add),
                         reads=[kg, 'COF', 'COB', cks[d]], writes=[cks[d]])
        for n_ in range(8):
            for d in range(2):
                i = tidx(n_, d)
                DST = SF if d == 0 else SB
                sb_ = sbf[d][n_ % 2]; kb = KB(sb_)
                P.op('act', lambda e, sb_=sb_, d=d: e.activation(sb_, cur[d], AF.Copy), reads=[cks[d]], writes=[kb])
                if n_ < 7:
                    g1 = pend[d]; kg = KB(g1)
                    if n_ < 6: pend[d] = ld_u(tidx(n_ + 1, d), d)
                    for h in range(4):
                        P.op('dve', lambda e, g1=g1, h=h, d=d: e.scalar_tensor_tensor(cvs[d][:, 2 * h:2 * h + 2, :], cvs[d][:, 2 * h:2 * h + 2, :], C512[:, 4 * d + h:4 * d + h + 1],
                                                                                    G3(g1)[:, 2 * h:2 * h + 2, :], ALU.mult, ALU.add),
                             reads=[kg, 'C512', cks[d]], writes=[cks[d]])
                dma(DST[i].rearrange("g p e -> p g e"), G3(sb_), reads=[kb])
        hgv = hg.rearrange("p (r s k c) -> p r s k c", r=4, s=2, k=8)
        for r in range(4):
            for s in range(2):
                dma(hgv[:, r, s, :, :], COUTH[r * 2048 + s * 1024:r * 2048 + (s + 1) * 1024, :].rearrange("(k p) c -> p k c", p=128), reads=['COUTH'], writes=['hg'])
        pv = pads.rearrange("p (s k c) -> p s k c", s=2, k=8)
        for side in range(2):
            for r in range(4):
                mcol = (18 if side == 0 else 22) + r
                if r == 0:
                    P.op('dve', lambda e, side=side, r=r, mcol=mcol: e.tensor_scalar(pv[:, side], hgv[:, r, 1 - side], CC[:, mcol:mcol + 1], None, ALU.mult), reads=['hg', 'CC'], writes=['pads'])
                else:
                    P.op('dve', lambda e, side=side, r=r, mcol=mcol: e.scalar_tensor_tensor(pv[:, side], hgv[:, r, 1 - side], CC[:, mcol:mcol + 1], pv[:, side], ALU.mult, ALU.add),
                         reads=['hg', 'CC', 'pads'], writes=['pads'])
        dma(YT[:, :, 1:16].rearrange("k p c -> p k c"), pv[:, 0, :, 1:16], reads=['pads'])
        dma(YT[:, :, 16 + T:16 + T + 15].rearrange("k p c -> p k c"), pv[:, 1, :, 0:15], reads=['pads'])

        stage_end(7)
        P.barrier(); AR.reset(); set_limit(set())
        qkb = [[AR.bf16(4096) for _ in range(4)] for _ in range(2)]; vt = AR.bf16(8192); rgt = AR.bf16(8192)
        sft = AR.bf16(4096); sbt = AR.bf16(4096); pts = [AR.bf16(2048), AR.bf16(2048)]; zz = AR.bf16(8192); zts = AR.bf16(4096)
        junk = AR.bf16(512); og = [AR.f32(512), AR.f32(512)]; stt = AR.f32(16 * 8)
        DMT = AR.f32(8192)
        dma(DMT, DMT_D, writes=['DMT'])
        V3 = lambda a, k: a.rearrange("p (k t) -> p k t", k=k)
        sfv, sbv = V3(sft, 8), V3(sbt, 8)
        vtv = vt.rearrange("p (s d) -> p s d", d=2048); rgv = rgt.rearrange("p (s d) -> p s d", d=2048)
        ptvs = [p_.rearrange("p (s c) -> p s c", c=512) for p_ in pts]; zv = zz.rearrange("p (s d) -> p s d", d=2048); ztv = V3(zts, 8)
        def ld_qk(ti):
            t0 = MAIN[ti][0]
            for n__, src in enumerate((QTR, QTF, QTB, KTR)):
                dma(V3(qkb[ti % 2][n__], 8), src[:, :, t0:t0 + 512].rearrange("k p t -> p k t"), writes=[KB(qkb[ti % 2][n__])])
        def ld_head(ti, h):
            t0 = MAIN[ti][0]
            dma(vtv[:, :, h * 512:(h + 1) * 512], VV[t0:t0 + 512, h * 512:(h + 1) * 512].rearrange("(s p) d -> p s d", p=128), writes=['vt_h%d' % h])
            dma(sfv[:, 2 * h:2 * h + 2, :], SF[ti][2 * h:2 * h + 2].rearrange("g p e -> p g e"), writes=['sf_h%d' % h])
            dma(sbv[:, 2 * h:2 * h + 2, :], SB[ti][2 * h:2 * h + 2].rearrange("g p e -> p g e"), writes=['sb_h%d' % h])
            dma(rgv[:, :, h * 512:(h + 1) * 512], RG[t0:t0 + 512, h * 512:(h + 1) * 512].rearrange("(s p) d -> p s d", p=128), writes=['rg_h%d' % h])
        ld_qk(0)
        for ti, (t0, W) in enumerate(MAIN):
            qr, qf, qb, kt = qkb[ti % 2]
            qrv, qfv, qbv, ktv = V3(qr, 8), V3(qf, 8), V3(qb, 8), V3(kt, 8)
            if ti + 1 < 8: ld_qk(ti + 1)
            if ti == 0:
                for h in range(4): ld_head(0, h)
            kz = KB(zz)
            klq = [KB(a_) for a_ in (qr, qf, qb, kt)]
            def emit_S(h):
                ptv = ptvs[h % 2]; kp = KB(pts[h % 2])
                for sc in range(4):
                    b = bank()
                    def f(e, sc=sc, b=b):
                        r = None
                        for dc in range(2):
                            r = e.matmul(PS[b], ktv[:, 2 * h + dc, sc * 128:(sc + 1) * 128], qrv[:, 2 * h + dc, :], start=(dc == 0), stop=(dc == 1))
                        return r
                    P.op('pe', f, reads=[klq], writes=['ps%d' % b])
                    P.op('dve', lambda e, sc=sc, b=b: e.tensor_tensor(ptv[:, sc, :], PS[b], DMT[:, h * 2048 + sc * 512:h * 2048 + (sc + 1) * 512], ALU.mult),
                         reads=['ps%d' % b, 'DMT'], writes=[kp])
            emit_S(0)
            for h in range(4):
                kl = klq + ['vt_h%d' % h, 'sf_h%d' % h, 'sb_h%d' % h]
                krg = 'rg_h%d' % h
                ptv = ptvs[h % 2]; kp = KB(pts[h % 2])
                if h + 1 < 4: emit_S(h + 1)
                for s in range(4):
                    b = bank(); idx = h * 4 + s
                    def f(e, h=h, s=s, b=b):
                        for sc in range(4):
                            e.matmul(PS[b], ptv[:, sc, s * 128:(s + 1) * 128], vtv[:, sc, h * 512:(h + 1) * 512], start=(sc == 0), stop=False)
                        r = None
                        for dc in range(2):
                            e.matmul(PS[b], qfv[:, 2 * h + dc, s * 128:(s + 1) * 128], sfv[:, 2 * h + dc, :], start=False, stop=False)
                        for dc in range(2):
                            r = e.matmul(PS[b], qbv[:, 2 * h + dc, s * 128:(s + 1) * 128], sbv[:, 2 * h + dc, :], start=False, stop=(dc == 1))
                        return r
                    P.op('pe', f, reads=[kl, kp], writes=['ps%d' % b])
                    sv = lambda c, idx=idx: stt[:, idx * 8 + c:idx * 8 + c + 1]
                    k1, k2, k3, k4, k5, k6, k7, k8 = ['stt%d_%d' % (idx, c_) for c_ in range(8)]
                    pk = 'ps%d' % b
                    P.op('act', lambda e, b=b, sv=sv: e.activation(junk, PS[b], AF.Copy, accum_out=sv(0)), reads=[pk], writes=[k1, 'junk'])
                    P.op('act', lambda e, b=b, sv=sv: e.activation(junk, PS[b], AF.Square, accum_out=sv(1)), reads=[pk], writes=[k2, 'junk'])
                    P.op('dve', lambda e, sv=sv: e.tensor_scalar(sv(2), sv(0), -1.0 / 512.0, None, ALU.mult), reads=[k1], writes=[k3])
                    P.op('dve', lambda e, sv=sv: e.tensor_tensor(sv(3), sv(2), sv(2), ALU.mult), reads=[k3], writes=[k4])
                    P.op('dve', lambda e, sv=sv: e.scalar_tensor_tensor(sv(4), sv(1), 1.0 / 512.0, sv(3), ALU.mult, ALU.subtract), reads=[k2, k4], writes=[k5])
                    P.op('act', lambda e, sv=sv: e.activation(sv(5), sv(4), AF.Ln, bias=EPS, scale=1.0), reads=[k5], writes=[k6])
                    P.op('act', lambda e, sv=sv: e.activation(sv(6), sv(5), AF.Exp, scale=-0.5), reads=[k6], writes=[k7])
                    o1 = og[idx % 2]; ko = KB(o1)
                    P.op('dve', lambda e, b=b, sv=sv, o1=o1: e.tensor_scalar(o1, PS[b], sv(2), sv(6), ALU.add, ALU.mult), reads=[pk, k3, k7], writes=[ko])
                    P.op('pool', lambda e, o1=o1, h=h, s=s: e.tensor_tensor(zv[:, s, h * 512:(h + 1) * 512], o1, rgv[:, s, h * 512:(h + 1) * 512], ALU.mult),
                         reads=[ko, krg], writes=[kz])
                if ti + 1 < 8: ld_head(ti + 1, h)
            kzt = KB(zts)
            for half in range(2):
                for s in range(4):
                    b = bank(); pb = PS[b].bitcast(BF16)
                    def f(e, s=s, half=half, pb=pb):
                        r = None
                        for q in range(8):
                            j = half * 8 + q
                            r = e.transpose(pb[:, q * 128:(q + 1) * 128], zv[:, s, j * 128:(j + 1) * 128], identb)
                        return r
                    P.op('pe', f, reads=[kz, 'identb'], writes=['ps%d' % b])
                    copy_any(('act', 'dve')[s % 2], ztv[:, :, s * 128:(s + 1) * 128],
                             pb[:, 0:1024].rearrange("p (q c) -> p q c", q=8), ['ps%d' % b], [kzt])
                dma(ZT[half * 8:(half + 1) * 8, :, t0:t0 + 512].rearrange("k p t -> p k t"), ztv, reads=[kzt])

        stage_end(8)
        def allocA(names):
            def a():
                c = {'st': [AR.f32(512) for _ in range(3)], 'sb': [AR.bf16(512) for _ in range(3)], 'tmp': [AR.f32(512) for _ in range(3)], 'n': [0], 'k': {}}
                for nme in names: c[nme] = [AR.f32(4096), AR.f32(4096)]
                return c
            return a
        def mkpre(pairs):
            def pre(ti, t0, W, gi, c):
                for nme, src in pairs:
                    r = c[nme][ti % 2].rearrange("p (k t) -> p k t", k=8); k = KB(c[nme][ti % 2]); c['k'][(nme, ti)] = k
                    dma(r, src[:, :, t0:t0 + 512].rearrange("k p t -> p k t"), writes=[k])
            return pre
        def ev_ret(ti, t0, W, gi, u, pss, c):
            i = c['n'][0] % 3; c['n'][0] += 1
            st = c['st'][i]; ks = KB(st); r = c['sga'][ti % 2].rearrange("p (k t) -> p k t", k=8)
            P.op('dve', lambda e: e.tensor_tensor(st, PS[pss[0]], r[:, u, :], ALU.mult), reads=['ps%d' % pss[0], c['k'][('sga', ti)]], writes=[ks])
            dma(M1[u, :, t0:t0 + 512], st, reads=[ks])
        gemm_fm(ZT, 16, [(8, [(w_ro, 0, 1.0, 'plain')])], MAIN, ev_ret, pre_tile=mkpre([('sga', SGA)]), extra_alloc=allocA(['sga']),
                slot=1, preloaded=False, next_w=(8, 8, [(w_co, 0, 1.0, 'plain')]))

        P.barrier(); AR.reset(); set_limit({0})
        DG = AR.bf16(248 * 128)
        ypb = [AR.bf16(8 * 544), AR.bf16(8 * 544)]; yc = AR.f32(4096); sqs = [AR.f32(512), AR.f32(512)]; mean = AR.f32(512); m2 = AR.f32(512)
        var = AR.f32(512); rs = AR.f32(512); tl = AR.f32(512); tps = [AR.f32(512) for _ in range(4)]; ct = AR.bf16(4096)
        ycv = V3(yc, 8)
        CW = lambda k, kc: SM[:, 144 + k * 8 + kc:145 + k * 8 + kc]
        kdg = KB(DG)
        for q4 in range(4):
            dma(DG[:, q4 * 7936:(q4 + 1) * 7936], DG_D[:, q4 * 7936:(q4 + 1) * 7936], writes=[kdg])
        def ld_y(ti):
            t0 = MAIN[ti][0]
            dma(ypb[ti % 2].rearrange("p (k t) -> p k t", k=8)[:, :, 0:542], YT[:, :, t0 + 1:t0 + 543].rearrange("k p t -> p k t"), writes=[KB(ypb[ti % 2])])
        ld_y(0)
        for ti, (t0, W) in enumerate(MAIN):
            yp = ypb[ti % 2].rearrange("p (k t) -> p k t", k=8); ky = KB(ypb[ti % 2])
            if ti + 1 < 8: ld_y(ti + 1)
            kyc = KB(yc)
            b1 = bank(); b2 = bank()
            for kc in range(8):
                b = bank()
                while b in (b1, b2): b = bank()
                def f(e, kc=kc, yp=yp, b=b):
                    for k in range(31):
                        j = k * 8 + kc
                        e.matmul(PS[b], DG[:, j * 128:(j + 1) * 128], yp[:, kc, k:k + 512], start=(k == 0), stop=(k == 30))
                P.op('pe', f, reads=[kdg, ky], writes=['ps%d' % b])
                P.op('act', lambda e, kc=kc, b=b: e.activation(ycv[:, kc, :], PS[b], AF.Identity, bias=SM[:, 120 + kc:121 + kc], scale=1.0),
                     reads=['ps%d' % b, 'SM'], writes=[kyc])
                sq_ = sqs[kc % 2]; ksq = KB(sq_)
                P.op('act', lambda e, kc=kc, sq_=sq_: e.activation(sq_, ycv[:, kc, :], AF.Square), reads=[kyc], writes=[ksq])
                P.op('pe', lambda e, kc=kc: e.matmul(PS[b1], ones, ycv[:, kc, :], start=(kc == 0), stop=(kc == 7)), reads=[kyc, 'ones'], writes=['ps%d' % b1])
                P.op('pe', lambda e, kc=kc, sq_=sq_: e.matmul(PS[b2], ones, sq_, start=(kc == 0), stop=(kc == 7)), reads=[ksq, 'ones'], writes=['ps%d' % b2])
            km = KB(mean); km2 = KB(m2); kv = KB(var); krs = KB(rs)
            P.op('act', lambda e, b1=b1: e.activation(mean, PS[b1], AF.Copy), reads=['ps%d' % b1], writes=[km])
            P.op('dve', lambda e: e.tensor_tensor(m2, mean, mean, ALU.mult), reads=[km], writes=[km2])
            P.op('dve', lambda e, b2=b2: e.tensor_tensor(var, PS[b2], m2, ALU.subtract), reads=['ps%d' % b2, km2], writes=[kv])
            rstd_from(var, 512, rs, tl, [kv], krs, KB(tl))
            ctv = V3(ct, 8); kct = KB(ct)
            for kc in range(8):
                ta = tps[(kc % 2) * 2]; tb_ = tps[(kc % 2) * 2 + 1]; ka = KB(ta); kb = KB(tb_); kc2 = KB(ta)
                P.op('dve', lambda e, kc=kc, ta=ta: e.tensor_tensor(ta, ycv[:, kc, :], mean, ALU.subtract), reads=[kyc, km], writes=[ka])
                P.op('dve', lambda e, ta=ta, tb_=tb_: e.tensor_tensor(tb_, ta, rs, ALU.mult), reads=[ka, krs], writes=[kb])
                P.op('dve', lambda e, kc=kc, ta=ta, tb_=tb_: e.tensor_scalar(ta, tb_, SM[:, 128 + kc:129 + kc], SM[:, 136 + kc:137 + kc], ALU.mult, ALU.add), reads=[kb, 'SM'], writes=[kc2])
                P.op('act', lambda e, kc=kc, ta=ta: e.activation(ctv[:, kc, :], ta, AF.Silu), reads=[kc2], writes=[kct])
            dma(CT[:, :, t0:t0 + 512].rearrange("k p t -> p k t"), ctv, reads=[kct])

        def ev_conv(ti, t0, W, gi, u, pss, c):
            i = c['n'][0] % 3; c['n'][0] += 1
            tmp = c['tmp'][i]; sb_ = c['sb'][i]; kt_ = KB(tmp); ks = KB(sb_)
            r = c['sgb'][ti % 2].rearrange("p (k t) -> p k t", k=8); m = c['m1'][ti % 2].rearrange("p (k t) -> p k t", k=8)
            P.op('dve', lambda e: e.tensor_tensor(tmp, PS[pss[0]], r[:, u, :], ALU.mult), reads=['ps%d' % pss[0], c['k'][('sgb', ti)]], writes=[kt_])
            P.op('pool', lambda e: e.tensor_tensor(sb_, tmp, m[:, u, :], ALU.add), reads=[kt_, c['k'][('m1', ti)]], writes=[ks])
            dma(MT[u, :, t0:t0 + 512], sb_, reads=[ks])
        gemm_fm(CT, 8, [(8, [(w_co, 0, 1.0, 'plain')])], MAIN, ev_conv, pre_tile=mkpre([('sgb', SGB), ('m1', M1)]), extra_alloc=allocA(['sgb', 'm1']),
                slot=0, preloaded=True)
        def ev_out(ti, t0, W, gi, u, pss, c):
            r = c['h1'][ti % 2].rearrange("p (k t) -> p k t", k=8); k = c['k'][('h1', ti)]
            P.op('dve', lambda e: e.scalar_tensor_tensor(r[:, u, :], PS[pss[0]], MODv(5, u, 0), r[:, u, :], ALU.mult, ALU.add),
                 reads=['ps%d' % pss[0], k, 'MODS'], writes=[k])
        def alloc_out():
            c = {'st': [AR.f32(512) for _ in range(3)], 'n': [0], 'k': {}, 'h1': [AR.f32(4096), AR.f32(4096)]}
            c['sq'] = AR.f32(4096); c['rs'] = AR.f32(512); c['xm'] = AR.bf16(4096)
            return c
        def post_out(ti, t0, W, gi, c):
            rb = c['h1'][ti % 2]; r = rb.rearrange("p (k t) -> p k t", k=8); k = c['k'][('h1', ti)]
            dma(H2[:, :, t0:t0 + 512].rearrange("k p t -> p k t"), r, reads=[k])
            norm_tile(rb, 512, lambda kc: DERv(0, 3, kc), lambda kc: MODv(6, kc, 0), XT, t0, k, c['sq'], c['rs'], c['st'], c['xm'])
        gemm_fm(MT, 8, [(8, [(w_o, 0, 1.0, 'plain')])], MAIN, ev_out, pre_tile=mkpre([('h1', H1)]), post_tile=post_out, extra_alloc=alloc_out,
                slot=1, preloaded=False, next_w=(8, 11, [(w2i, 0, 1.0, 'plain'), (w2i, DFF, 1.0, 'plain')]))

        stage_end(9)
        ffn(XT, w2i, w2o, MAIN, H2, 4, None, final=True)
        P.emit(nc)
    return nc


def _host_inputs(x, c, ctx, c_ctx, w_mod, b_mod, norm_ffn1, w_ffn1_in, w_ffn1_out, norm_mix, w_in,
                 ret_decay_f, ret_decay_b, ret_gn_w, w_ret_out, conv_w, conv_b, conv_ln_w, conv_ln_b,
                 w_conv_out, w_out, norm_ffn2, w_ffn2_in, w_ffn2_out, final_norm):
    f = lambda a: np.ascontiguousarray(np.asarray(a, dtype=np.float32))
    shared = dict(w_mod=f(w_mod[0]), w_ffn1_in=f(w_ffn1_in[0]), w_ffn1_out=f(w_ffn1_out[0]), w_in=f(w_in[0]),
                  w_ret_out=f(w_ret_out[0]), w_conv_out=f(w_conv_out[0]), w_out=f(w_out[0]),
                  w_ffn2_in=f(w_ffn2_in[0]), w_ffn2_out=f(w_ffn2_out[0]), gnw=f(ret_gn_w[0]).reshape(1, 2048),
                  decay=np.concatenate([f(ret_decay_f[0]), f(ret_decay_b[0])]).reshape(1, 8),
                  ident=np.eye(128, dtype=np.float32))
    p = np.arange(128, dtype=np.float32)[:, None]
    pc = np.zeros((128, NPC), np.float32)
    sc = np.arange(4, dtype=np.float32)[None, :]
    pc[:, 0:4] = 511 - 128 * sc - p; pc[:, 4:8] = 128 * sc + p
    i8 = np.arange(8, dtype=np.float32)[None, :]
    pc[:, 8:16] = 512 * (7 - i8); pc[:, 16:24] = 512 * i8
    cc = np.arange(512, dtype=np.float32)[None, :]
    pc[:, 24:536] = cc + 1; pc[:, 536:1048] = 512 - cc
    for s in range(4):
        pc[:, 1048 + s * 512:1048 + (s + 1) * 512] = cc - 128 * s - p
    shared['pconst'] = pc
    half = 128
    inv = (10000.0 ** (-np.arange(0, half, 2, dtype=np.float32) / half)).astype(np.float32)
    invp = np.concatenate([inv, inv])[:, None]
    maps = []
    for core in range(8):
        b = core // 4; s = core % 4
        d = dict(shared)
        d['xin'] = np.ascontiguousarray(np.concatenate([f(x[b, s * T:(s + 1) * T]), f(ctx[b])], 0))
        sm = np.zeros((512, 128), np.float32)
        sm[0:72] = f(b_mod[0]).reshape(72, 128)
        for r0, v in ((72, c[b]), (80, c_ctx), (88, norm_ffn1[0]), (96, norm_mix[0]), (104, norm_ffn2[0]), (112, final_norm),
                      (120, conv_b[0]), (128, conv_ln_w[0]), (136, conv_ln_b[0])):
            sm[r0:r0 + 8] = f(v).reshape(8, 128)
        sm[144:144 + 248] = f(conv_w[0]).reshape(31 * 8, 128)
        d['smalls'] = sm
        tpos = (s * T + np.arange(T)).astype(np.float32)
        rows = np.floor(tpos / 64.0).astype(np.float32); cols = (tpos - rows * 64).astype(np.float32)
        tab = np.zeros((4, 128, TA), np.float32)
        ar = (rows[None, :] * invp).astype(np.float32); ac = (cols[None, :] * invp).astype(np.float32)
        tab[0, :, :T] = np.cos(ar); tab[1, :, :T] = np.sin(ar); tab[2, :, :T] = np.cos(ac); tab[3, :, :T] = np.sin(ac)
        tab[0, :, T:] = 1.0; tab[2, :, T:] = 1.0
        d['ropetab'] = tab
        cc_ = np.zeros((1, 32), np.float32)
        for r in range(4):
            if r < s: cc_[0, r] = T * (s - 1 - r); cc_[0, 4 + r] = 1.0
            if r > s: cc_[0, 8 + r] = T * (r - s - 1); cc_[0, 12 + r] = 1.0
            if r == s - 1: cc_[0, 18 + r] = 1.0
            if r == s + 1: cc_[0, 22 + r] = 1.0
        cc_[0, 16] = T * s; cc_[0, 17] = T * (3 - s)
        d['cconst'] = cc_
        maps.append(d)
    return maps


_NC = [None]


def kernel(**inputs):
    maps = _host_inputs(**inputs)
    if _NC[0] is None:
        _NC[0] = build_nc()
    res = run_bass_kernel_spmd(_NC[0], maps, core_ids=list(range(8)))
    out = np.zeros((2, 4 * T, DM_), np.float32)
    for core in range(8):
        out[core // 4, (core % 4) * T:(core % 4 + 1) * T] = res.results[core]["out"]
    return out
```
